# Optimizing a Trainium2 kernel written in Bass

```python
import jax, jax.numpy as jnp
from jax import lax
import numpy as np

D_MODEL = 2048
BATCH = 4
SEQ = 4096
DEPTH = 4

NORM_EPS = 1e-6
CHUNK = 128
A_EXPAND = 2
A_WIDTH = A_EXPAND * D_MODEL
A_GROUPS = 16
A_GROUP_DIM = A_WIDTH // A_GROUPS
B_WINDOWS = (128, 512, 2048)
B_DILATIONS = (1, 4, 16)
B_NGROUPS = len(B_WINDOWS)
B_HEAD_DIM = 128
B_HEADS = D_MODEL // B_HEAD_DIM
B_WIDTH = B_HEADS * B_HEAD_DIM
B_TOTAL_HEADS = B_NGROUPS * B_HEADS
Q_BLOCK = 128
NEG_INF = -1e30
N_A_LAYERS = (DEPTH + 1) // 2
N_B_LAYERS = DEPTH // 2

kernel_name = "hybrid_gmlp_dilated_swa_alibi"


def rms_norm(x, g):
    x32 = x.astype(jnp.float32)
    y = x32 * lax.rsqrt(jnp.mean(x32 * x32, axis=-1, keepdims=True) + NORM_EPS)
    return (y * g.astype(jnp.float32)).astype(x.dtype)


def layer_norm(x, g, b):
    x32 = x.astype(jnp.float32)
    mu = jnp.mean(x32, axis=-1, keepdims=True)
    var = jnp.mean(jnp.square(x32 - mu), axis=-1, keepdims=True)
    y = (x32 - mu) * lax.rsqrt(var + NORM_EPS)
    return (y * g.astype(jnp.float32) + b.astype(jnp.float32)).astype(x.dtype)


def alibi_slopes():
    n = np.arange(1, B_TOTAL_HEADS + 1, dtype=np.float32)
    return jnp.asarray(np.power(np.float32(2.0), -8.0 * n / B_TOTAL_HEADS).astype(np.float32))


def mixer_a(h, w_in, ln_g, ln_b, w_s, b_s, w_out):
    b, s, _ = h.shape
    p = h @ w_in
    uv = jax.nn.gelu(p[..., :2 * A_WIDTH], approximate=False)
    u, v = uv[..., :A_WIDTH], uv[..., A_WIDTH:]
    z = p[..., 2 * A_WIDTH:]
    v = layer_norm(v, ln_g, ln_b)
    vc = v.reshape(b, s // CHUNK, CHUNK, A_GROUPS, A_GROUP_DIM)
    causal = jnp.tril(jnp.ones((CHUNK, CHUNK), dtype=w_s.dtype))
    ws = w_s * causal[None]
    mixed = jnp.einsum('gts,bnsgc->bntgc', ws, vc) + b_s.T[None, None, :, :, None]
    gated = u * mixed.reshape(b, s, A_WIDTH)
    return (gated * jax.nn.silu(z)) @ w_out


def dilated_window_attention(q, k, v, window, dilation, slopes):
    b, s, h, dh = q.shape
    L = s // dilation
    bd = b * dilation

    def to_sub(t):
        return t.reshape(b, L, dilation, h, dh).transpose(0, 2, 1, 3, 4).reshape(bd, L, h, dh)

    qs, ks, vs = to_sub(q), to_sub(k), to_sub(v)
    span = window // dilation
    n_prev = -(-span // Q_BLOCK)
    nb = -(-L // Q_BLOCK)
    pad = nb * Q_BLOCK - L
    qs = jnp.pad(qs, ((0, 0), (0, pad), (0, 0), (0, 0)))
    ks = jnp.pad(ks, ((0, 0), (n_prev * Q_BLOCK, pad), (0, 0), (0, 0)))
    vs = jnp.pad(vs, ((0, 0), (n_prev * Q_BLOCK, pad), (0, 0), (0, 0)))
    qb = qs.reshape(bd, nb, Q_BLOCK, h, dh)
    kb = ks.reshape(bd, nb + n_prev, Q_BLOCK, h, dh)
    vb = vs.reshape(bd, nb + n_prev, Q_BLOCK, h, dh)
    kw = jnp.concatenate([kb[:, j:j + nb] for j in range(n_prev + 1)], axis=2)
    vw = jnp.concatenate([vb[:, j:j + nb] for j in range(n_prev + 1)], axis=2)
    kwid = (n_prev + 1) * Q_BLOCK

    scale = 1.0 / np.sqrt(dh)
    scores = jnp.einsum('bnqhd,bnkhd->bnhqk', qb, kw,
                        preferred_element_type=jnp.float32) * scale
    qi = jnp.arange(Q_BLOCK)[:, None]
    kc = jnp.arange(kwid)[None, :]
    dist = qi + n_prev * Q_BLOCK - kc
    key_idx = jnp.arange(nb)[:, None, None] * Q_BLOCK - n_prev * Q_BLOCK + kc[None]
    valid = (dist >= 0)[None] & (dist <= span)[None] & (key_idx >= 0)
    bias = -slopes[:, None, None] * (dist * dilation).astype(jnp.float32)[None]
    scores = jnp.where(valid[None, :, None], scores + bias[None, None], NEG_INF)
    m = jnp.max(scores, axis=-1, keepdims=True)
    e = jnp.exp(scores - m)
    den = jnp.sum(e, axis=-1)
    o = jnp.einsum('bnhqk,bnkhd->bnqhd', e, vw.astype(jnp.float32))
    o = o / den.transpose(0, 1, 3, 2)[..., None]
    lse = (m[..., 0] + jnp.log(den)).transpose(0, 1, 3, 2)

    o = o.reshape(bd, nb * Q_BLOCK, h, dh)[:, :L]
    lse = lse.reshape(bd, nb * Q_BLOCK, h)[:, :L]
    o = o.reshape(b, dilation, L, h, dh).transpose(0, 2, 1, 3, 4).reshape(b, s, h, dh)
    lse = lse.reshape(b, dilation, L, h).transpose(0, 2, 1, 3).reshape(b, s, h)
    return o, lse


def mixer_b(h, w_in, w_out, slopes):
    b, s, _ = h.shape
    p = h @ w_in
    outs, lses = [], []
    for g in range(B_NGROUPS):
        off = 3 * g * B_WIDTH
        q = p[..., off:off + B_WIDTH].reshape(b, s, B_HEADS, B_HEAD_DIM)
        k = p[..., off + B_WIDTH:off + 2 * B_WIDTH].reshape(b, s, B_HEADS, B_HEAD_DIM)
        v = p[..., off + 2 * B_WIDTH:off + 3 * B_WIDTH].reshape(b, s, B_HEADS, B_HEAD_DIM)
        o, lse = dilated_window_attention(q, k, v, B_WINDOWS[g], B_DILATIONS[g],
                                          slopes[g * B_HEADS:(g + 1) * B_HEADS])
        outs.append(o)
        lses.append(lse)
    wts = jax.nn.softmax(jnp.stack(lses, axis=0), axis=0)
    o = jnp.sum(wts[..., None] * jnp.stack(outs, axis=0), axis=0)
    o = o.reshape(b, s, B_WIDTH).astype(h.dtype)
    z = p[..., 3 * B_NGROUPS * B_WIDTH:]
    return (o * jax.nn.silu(z)) @ w_out


def setup_inputs(seed: int = 0) -> dict:
    key = jax.random.key(seed)
    ks = jax.random.split(key, 12)
    f32 = jnp.float32
    nrm = lambda k, shape: jax.random.normal(k, shape, dtype=f32)
    return {
        "x": nrm(ks[0], (BATCH, SEQ, D_MODEL)),
        "a_norm_g": 1.0 + 0.02 * nrm(ks[1], (N_A_LAYERS, D_MODEL)),
        "a_w_in": nrm(ks[2], (N_A_LAYERS, D_MODEL, 3 * A_WIDTH)) * D_MODEL ** -0.5,
        "a_ln_g": 1.0 + 0.02 * nrm(ks[3], (N_A_LAYERS, A_WIDTH)),
        "a_ln_b": 0.02 * nrm(ks[4], (N_A_LAYERS, A_WIDTH)),
        "a_w_s": nrm(ks[5], (N_A_LAYERS, A_GROUPS, CHUNK, CHUNK)) * CHUNK ** -0.5,
        "a_b_s": 1.0 + 0.02 * nrm(ks[6], (N_A_LAYERS, A_GROUPS, CHUNK)),
        "a_w_out": nrm(ks[7], (N_A_LAYERS, A_WIDTH, D_MODEL)) * A_WIDTH ** -0.5,
        "b_norm_g": 1.0 + 0.02 * nrm(ks[8], (N_B_LAYERS, D_MODEL)),
        "b_w_in": nrm(ks[9], (N_B_LAYERS, D_MODEL, 3 * B_NGROUPS * B_WIDTH + B_WIDTH)) * D_MODEL ** -0.5,
        "b_w_out": nrm(ks[10], (N_B_LAYERS, B_WIDTH, D_MODEL)) * B_WIDTH ** -0.5,
        "final_norm_g": 1.0 + 0.02 * nrm(ks[11], (D_MODEL,)),
    }


def reference(x, a_norm_g, a_w_in, a_ln_g, a_ln_b, a_w_s, a_b_s, a_w_out,
              b_norm_g, b_w_in, b_w_out, final_norm_g):
    slopes = alibi_slopes()
    for i in range(DEPTH):
        j = i // 2
        if i % 2 == 0:
            h = rms_norm(x, a_norm_g[j])
            x = x + mixer_a(h, a_w_in[j], a_ln_g[j], a_ln_b[j], a_w_s[j], a_b_s[j], a_w_out[j])
        else:
            h = rms_norm(x, b_norm_g[j])
            x = x + mixer_b(h, b_w_in[j], b_w_out[j], slopes)
    return rms_norm(x, final_norm_g)
```

```python
import numpy as np
import concourse.bass as bass
import concourse.mybir as mybir
from contextlib import ExitStack
F32 = mybir.dt.float32; BF16 = mybir.dt.bfloat16
AF = mybir.ActivationFunctionType
ALU = mybir.AluOpType
AX = mybir.AxisListType


_UID = [0]


def uniq(name):
    _UID[0] += 1
    return "%s_u%d" % (name, _UID[0])


class Buf:
    __slots__ = ("name", "w", "r")

    def __init__(self, name):
        self.name = name
        self.w = None
        self.r = {}


class FW:
    def __init__(self, nc, es, n_dma_sems=50):
        self.nc = nc
        self.es = es
        self.eng = {"pe": nc.tensor, "act": nc.scalar, "dve": nc.vector, "pool": nc.gpsimd, "sp": nc.sync}
        self.sem = {}
        self.cnt = {}
        for k in ("pe", "act", "dve", "pool"):
            self.sem[k] = es.enter_context(nc.semaphore("sem_" + k))
            self.cnt[k] = 0
        self.n_dma = 0
        self.waited = {k: {} for k in self.eng}
        self.dma_free = []
        self.n_dma_sems = n_dma_sems
        self.dma_rr = 0
        self.q_rr = {}
        self.keymap = {}
        self.ninst = 0

    def _need(self, e, deps):
        need = {}
        for d in deps:
            if d is None:
                continue
            k, v = d
            if need.get(k, 0) < v:
                need[k] = v
        w = self.waited[e]
        eng = self.eng[e]
        for k, v in need.items():
            if k == e and e == "pe":
                continue
            if w.get(k, 0) >= v:
                continue
            eng.wait_ge(self.sem[k], v)
            w[k] = v
            self.ninst += 1

    def _deps(self, reads, writes):
        deps = []
        for b in reads:
            deps.append(b.w)
        for b in writes:
            deps.append(b.w)
            for k, v in b.r.items():
                deps.append((k, v))
        return deps

    def _mark(self, ev, reads, writes):
        k, v = ev
        for b in reads:
            b.r[k] = v
        for b in writes:
            b.w = ev
            b.r = {}

    def op(self, e, fn, reads=(), writes=()):
        self._need(e, self._deps(reads, writes))
        ins = fn(self.eng[e])
        self.cnt[e] += 1
        ins.then_inc(self.sem[e], 1)
        self._mark((e, self.cnt[e]), reads, writes)
        self.ninst += 1
        return ins

    def dma(self, q, out, in_, reads=(), writes=(), semkey=None):
        self._need(q, self._deps(reads, writes))
        if semkey is None:
            raise ValueError("dma needs a semkey (one in-flight DMA per key)")
        if semkey not in self.keymap:
            assert len(self.keymap) < self.n_dma_sems, "out of DMA semaphores"
            k = "dma%d" % len(self.keymap)
            self.keymap[semkey] = k
            self.sem[k] = self.es.enter_context(self.nc.semaphore("sem_" + k))
            self.cnt[k] = 0
        semkey = self.keymap[semkey]
        ins = self.eng[q].dma_start(out=out, in_=in_)
        self.cnt[semkey] += 16
        ins.then_inc(self.sem[semkey], 16)
        self._mark((semkey, self.cnt[semkey]), reads, writes)
        self.ninst += 1
        return ins

    def finish(self, bufs, e="sp"):
        self._need(e, [b.w for b in bufs])


def fw_barrier(fw):
    evs = [(k, v) for k, v in fw.cnt.items() if v > 0]
    for e in ("pe", "act", "dve", "pool", "sp"):
        w = fw.waited[e]
        for k, v in evs:
            if w.get(k, 0) >= v:
                continue
            fw.eng[e].wait_ge(fw.sem[k], v)
            w[k] = v
FW.barrier = fw_barrier


D = 2048
AW = 4096
EPS = 1e-6
NCHUNK_IN_A = 24


class Ctx:
    pass


def setup_common(nc, es, fw, T):
    c = Ctx()
    c.nc, c.es, c.fw, c.T = nc, es, fw, T
    c.NT = T // 128
    sb = lambda name, shape, dt: es.enter_context(nc.sbuf_tensor(uniq(name), shape, dt))
    c.sb = sb
    c.banks = [es.enter_context(nc.psum_tensor("bank%d" % i, [128, 512], F32)) for i in range(8)]
    c.bbuf = [Buf("bank%d" % i) for i in range(8)]
    c.bank_rr = 0
    c.idf = sb("idf", [128, 128], F32); c.b_idf = Buf("idf")
    c.idn = sb("idn", [128, 128], BF16); c.b_idn = Buf("idn")
    fw.op("pool", lambda e: e.memset(c.idf[:], 1.0), writes=[c.b_idf])
    fw.op("pool", lambda e: e.affine_select(out=c.idf[:], in_=c.idf[:], pattern=[[-1, 128]], compare_op=ALU.is_equal,
                                            fill=0.0, base=0, channel_multiplier=1), reads=[c.b_idf], writes=[c.b_idf])
    fw.op("dve", lambda e: e.tensor_copy(out=c.idn[:], in_=c.idf[:]), reads=[c.b_idf], writes=[c.b_idn])
    c.bX = [Buf("X%d" % i) for i in range(c.NT)]
    c.pump = ConvPump(c)
    c.pump_rate = 0
    return c


def next_bank(c):
    i = c.bank_rr
    c.bank_rr = (i + 1) % 8
    return c.banks[i], c.bbuf[i]


class ConvPump:
    def __init__(self, c):
        self.c = c
        self.q = []
        self.pos = 0
        self.hist = []

    def add(self, dst, src, buf):
        self.q.append((dst, src, buf))
        return len(self.q)

    def pump(self, n, paced=True):
        fw = self.c.fw
        if paced and n > 0 and self.pos < len(self.q) and fw.cnt["pe"] > 0:
            fw._need("pool", [("pe", fw.cnt["pe"])])
        while n > 0 and self.pos < len(self.q):
            dst, src, buf = self.q[self.pos]
            self.pos += 1
            n -= 1
            if len(self.hist) >= 2:
                fw._need("pool", [self.hist[-2]])
            fw.dma("pool", dst, src, writes=[buf], semkey="conv%d" % (self.pos % 3))
            self.hist.append(buf.w)

    def ensure(self, upto):
        if self.pos < upto:
            self.pump(upto - self.pos, paced=False)


def convert_weights(c, w_ap, rows, cols, name, order=None):
    nc = c.nc
    nkh = rows // 2048
    ncc = cols // 512
    wb = nc.dram_tensor(name, [nkh, ncc, 128, 16 * 512], BF16).ap()
    bufs = {}
    if order is None:
        order = [(kh, cc) for kh in range(nkh) for cc in range(ncc)]
    end = 0
    for (kh, cc) in order:
        b = Buf("%s_%d_%d" % (name, kh, cc))
        src = w_ap[kh * 2048:(kh + 1) * 2048, cc * 512:(cc + 1) * 512].rearrange("(k p) n -> p k n", p=128)
        dst = wb[kh, cc].rearrange("p (k n) -> p k n", k=16)
        end = c.pump.add(dst, src, b)
        bufs[(kh, cc)] = b
    return wb, bufs, end


def setup_A(c, es):
    sb = lambda name, shape, dt: es.enter_context(c.nc.sbuf_tensor(uniq(name), shape, dt))
    a = Ctx()
    a.G = 4
    a.xt = sb("a_xt", [128, D], F32); a.b_xt = Buf("a_xt")
    a.hb = sb("a_hb", [128, D], BF16); a.b_hb = Buf("a_hb")
    a.st = sb("a_st", [128, 16], F32); a.b_st = Buf("a_st")
    a.hTs = [sb("a_hT%d" % i, [128, 16, 512], BF16) for i in range(2)]; a.b_hTs = [Buf("a_hT%d" % i) for i in range(2)]
    a.g_bc = sb("a_gbc", [128, D], F32); a.b_gbc = Buf("a_gbc")
    a.lng = sb("a_lng", [128, AW], F32); a.b_lng = Buf("a_lng")
    a.lnb = sb("a_lnb", [128, AW], F32); a.b_lnb = Buf("a_lnb")
    a.wsf = a.xt[:].rearrange("p (g s) -> p g s", g=16); a.b_wsf = a.b_xt
    a.wsb = a.hb[:].rearrange("p (g s) -> p g s", g=16); a.b_wsb = a.b_hb
    a.wsT = sb("a_wsT", [128, 16, 128], BF16); a.b_wsT = Buf("a_wsT")
    a.bs = sb("a_bs", [128, 16], F32); a.b_bs = Buf("a_bs")
    a.W = [sb("a_W%d" % i, [128, 16, 512], BF16) for i in range(3)]
    a.b_W = [Buf("a_W%d" % i) for i in range(3)]
    a.w_rr = 0
    a.v = [sb("a_v%d" % i, [128, AW], BF16) for i in range(4)]; a.b_v = [[Buf("a_v%d_%d" % (i, q)) for q in range(8)] for i in range(4)]
    a.gtc = [sb("a_gtc%d" % i, [128, 512], BF16) for i in range(3)]; a.b_gtc = [Buf("a_gtc%d" % i) for i in range(3)]
    a.gtr = 0
    a.t3 = [sb("a_t3_%d" % i, [128, 512], F32) for i in range(2)]; a.b_t3 = [Buf("a_t3_%d" % i) for i in range(2)]
    a.w1 = sb("a_w1", [128, 16], F32); a.b_w1 = Buf("a_w1")
    a.stats = [sb("a_stats%d" % i, [128, 8, 6], F32) for i in range(4)]; a.b_stats = [Buf("a_stats%d" % i) for i in range(4)]
    a.tmp32s = [sb("a_tmp32_%d" % i, [128, 512], F32) for i in range(3)]; a.b_tmp32s = [Buf("a_tmp32_%d" % i) for i in range(3)]
    a.tmr = 0
    a.mvs = [sb("a_mv%d" % i, [128, 8], F32) for i in range(4)]; a.b_mvs = [Buf("a_mv%d" % i) for i in range(4)]
    a.t1 = [sb("a_t1_%d" % i, [128, 512], F32) for i in range(2)]; a.b_t1 = [Buf("a_t1_%d" % i) for i in range(2)]
    a.t2 = [sb("a_t2_%d" % i, [128, 512], F32) for i in range(2)]; a.b_t2 = [Buf("a_t2_%d" % i) for i in range(2)]
    a.xr = [sb("a_xr%d" % i, [128, 512], F32) for i in range(4)]; a.b_xr = [Buf("a_xr%d" % i) for i in range(4)]
    a.xrr = 0
    a.rr = 0
    return a


def load_W(c, a, wb, wbufs, kh, cc):
    fw = c.fw
    i = a.w_rr
    a.w_rr = (i + 1) % 3
    fw.dma("sp", a.W[i][:].rearrange("p k n -> p (k n)"), wb[kh, cc], reads=[wbufs[(kh, cc)]], writes=[a.b_W[i]], semkey="A_W%d" % i)
    return a.W[i], a.b_W[i]


def rms_pre(c, a, x_src, bX, gbc, b_gbc):
    fw = c.fw
    fw.dma("sp", a.xt[:], x_src, reads=[bX], writes=[a.b_xt], semkey="xt")
    fw.op("act", lambda e: e.activation(out=a.hb[:], in_=a.xt[:], func=AF.Square, scale=float(D ** -0.5),
                                        accum_out=a.st[:, 0:1]), reads=[a.b_xt], writes=[a.b_hb, a.b_st])
    fw.op("act", lambda e: e.activation(out=a.st[:, 1:2], in_=a.st[:, 0:1], func=AF.Sqrt, bias=EPS, scale=1.0),
          reads=[a.b_st], writes=[a.b_st])
    fw.op("dve", lambda e: e.reciprocal(out=a.st[:, 2:3], in_=a.st[:, 1:2]), reads=[a.b_st], writes=[a.b_st])
    fw.op("dve", lambda e: e.scalar_tensor_tensor(out=a.hb[:], in0=a.xt[:], scalar=a.st[:, 2:3], in1=gbc[:],
                                                  op0=ALU.mult, op1=ALU.mult),
          reads=[a.b_xt, a.b_st, b_gbc], writes=[a.b_hb])


def rms_tr(c, a, tile_in_group, hT, b_hT):
    fw = c.fw
    t0 = tile_in_group * 128
    for half in range(2):
        bank, bb = next_bank(c)
        pv = bank[:].bitcast(BF16).rearrange("p (k t) -> p k t", k=8)
        for k in range(8):
            kk = half * 8 + k
            fw.op("pe", lambda e: e.transpose(out=pv[:, k, :], in_=a.hb[:, kk * 128:(kk + 1) * 128], identity=c.idn[:]),
                  reads=[a.b_hb, c.b_idn], writes=[bb])
        if half == 0:
            fw.op("act", lambda e: e.activation(out=hT[:, half * 8:(half + 1) * 8, t0:t0 + 128], in_=pv, func=AF.Copy),
                  reads=[bb], writes=[b_hT])
        else:
            fw.op("dve", lambda e: e.tensor_copy(out=hT[:, half * 8:(half + 1) * 8, t0:t0 + 128], in_=pv),
                  reads=[bb], writes=[b_hT])


def rms_to_hT(c, a, x_src, bX, tile_in_group, hT, b_hT, gbc, b_gbc, hw=512):
    rms_pre(c, a, x_src, bX, gbc, b_gbc)
    rms_tr(c, a, tile_in_group, hT, b_hT)


def layer_A_consts(c, a, j, P):
    fw = c.fw
    fw.dma("sp", a.g_bc[:], P["a_norm_g"][j].partition_broadcast(128), writes=[a.b_gbc], semkey="gbc")
    fw.dma("sp", a.lng[:], P["a_ln_g"][j].partition_broadcast(128), writes=[a.b_lng], semkey="lng")
    fw.dma("sp", a.lnb[:], P["a_ln_b"][j].partition_broadcast(128), writes=[a.b_lnb], semkey="lnb")
    fw.dma("sp", a.wsf, P["a_w_s"][j].rearrange("g t s -> t g s"), writes=[a.b_wsf], semkey="xt")
    with c.nc.allow_non_contiguous_dma(reason="tiny bias transpose"):
        fw.dma("sp", a.bs[:], P["a_b_s"][j].rearrange("g t -> t g"), writes=[a.b_bs], semkey="bs")
    fw.op("pool", lambda e: e.affine_select(out=a.wsf, in_=a.wsf, pattern=[[0, 16], [-1, 128]],
                                            compare_op=ALU.is_ge, fill=0.0, base=0, channel_multiplier=1),
          reads=[a.b_wsf], writes=[a.b_wsf])
    fw.op("dve", lambda e: e.tensor_copy(out=a.wsb, in_=a.wsf), reads=[a.b_wsf], writes=[a.b_wsb])
    fw.op("dve", lambda e: e.tensor_reduce(out=a.w1[:], in_=a.wsf, axis=AX.X, op=ALU.add), reads=[a.b_wsf], writes=[a.b_w1])
    for g in range(16):
        gs = slice(g * 256, (g + 1) * 256)
        fw.op("dve", lambda e: e.tensor_scalar(out=a.lnb[:, gs], in0=a.lnb[:, gs], scalar1=a.w1[:, g:g + 1], scalar2=a.bs[:, g:g + 1],
                                               op0=ALU.mult, op1=ALU.add), reads=[a.b_lnb, a.b_w1, a.b_bs], writes=[a.b_lnb])
    for half in range(2):
        bank, bb = next_bank(c)
        pv = bank[:].bitcast(BF16).rearrange("p (k t) -> p k t", k=8)
        for k in range(8):
            g = half * 8 + k
            fw.op("pe", lambda e: e.transpose(out=pv[:, k, :], in_=a.wsb[:, g, :], identity=c.idn[:]),
                  reads=[a.b_wsb, c.b_idn], writes=[bb])
        fw.op("dve", lambda e: e.tensor_copy(out=a.wsT[:, half * 8:(half + 1) * 8, :], in_=pv), reads=[bb], writes=[a.b_wsT])


def phase_A(c, a, j, P, x_in, x_out, wb_in, wbufs_in, wb_out, wbufs_out):
    fw = c.fw
    NT = c.NT
    ngroups = NT // 4
    layer_A_consts(c, a, j, P)
    Bm, b_Bm = a.lnb, a.b_lnb

    def rms_tile(gi, i):
        t = gi * 4 + i
        rms_to_hT(c, a, x_in[t * 128:(t + 1) * 128, :], c.bX[t], i, a.hTs[gi % 2], a.b_hTs[gi % 2], a.g_bc, a.b_gbc)

    def mm16(bank, bb, lhs_of_k, W, bW, extra_reads, start_first=True, stop_last=True):
        for k in range(16):
            fw.op("pe", lambda e: e.matmul(bank[:, :], lhsT=lhs_of_k(k), rhs=W[:, k, :],
                                           start=(start_first and k == 0), stop=(stop_last and k == 15)),
                  reads=[bW] + (extra_reads(k) if callable(extra_reads) else extra_reads), writes=[bb])

    for i in range(4):
        rms_tile(0, i)
    for gi in range(ngroups):
        hT, b_hT = a.hTs[gi % 2], a.b_hTs[gi % 2]
        for cidx in range(8):
            W, bW = load_W(c, a, wb_in, wbufs_in, 0, 8 + cidx)
            for i in range(4):
                bank, bb = next_bank(c)
                mm16(bank, bb, lambda k: hT[:, k, i * 128:(i + 1) * 128], W, bW, [b_hT])
                vs = a.v[i][:, cidx * 512:(cidx + 1) * 512]
                fw.op("act", lambda e: e.activation(out=vs, in_=bank[:, :], func=AF.Gelu), reads=[bb], writes=[a.b_v[i][cidx]])
                fw.op("dve", lambda e: e.bn_stats(out=a.stats[i][:, cidx, :], in_=vs), reads=[a.b_v[i][cidx]], writes=[a.b_stats[i]])
        for i in range(4):
            mv, bmv = a.mvs[i], a.b_mvs[i]
            fw.op("dve", lambda e: e.bn_aggr(out=mv[:, 0:2], in_=a.stats[i][:]), reads=[a.b_stats[i]], writes=[bmv])
        for i in range(4):
            mv, bmv = a.mvs[i], a.b_mvs[i]
            fw.op("act", lambda e: e.activation(out=mv[:, 2:3], in_=mv[:, 1:2], func=AF.Sqrt, bias=EPS, scale=1.0),
                  reads=[bmv], writes=[bmv])
        for i in range(4):
            mv, bmv = a.mvs[i], a.b_mvs[i]
            fw.op("dve", lambda e: e.reciprocal(out=mv[:, 3:4], in_=mv[:, 2:3]), reads=[bmv], writes=[bmv])
        for i in range(4):
            mv, bmv = a.mvs[i], a.b_mvs[i]
            fw.op("dve", lambda e: e.tensor_scalar(out=mv[:, 4:5], in0=mv[:, 0:1], scalar1=-1.0, scalar2=mv[:, 3:4],
                                                   op0=ALU.mult, op1=ALU.mult), reads=[bmv], writes=[bmv])
        for i in range(4):
            mv, bmv = a.mvs[i], a.b_mvs[i]
            for q in range(8):
                sl = slice(q * 512, (q + 1) * 512)
                tm, btm = a.tmp32s[a.tmr], a.b_tmp32s[a.tmr]
                a.tmr = (a.tmr + 1) % 3
                fw.op("act", lambda e: e.activation(out=tm[:], in_=a.v[i][:, sl], func=AF.Identity,
                                                    bias=mv[:, 4:5], scale=mv[:, 3:4]),
                      reads=[a.b_v[i][q], bmv], writes=[btm])
                fw.op("dve", lambda e: e.tensor_tensor(out=a.v[i][:, sl], in0=tm[:], in1=a.lng[:, sl], op=ALU.mult),
                      reads=[btm, a.b_lng], writes=[a.b_v[i][q]])
        c.pump.pump(c.pump_rate)
        pend = []

        def flush_tr():
            while pend:
                i_, cidx_, gk = pend.pop(0)
                bank, bb = next_bank(c)
                pv = bank[:].bitcast(BF16)[:, 0:512].rearrange("p (k t) -> p k t", k=4)
                for k in range(4):
                    fw.op("pe", lambda e: e.transpose(out=pv[:, k, :], in_=a.gtc[gk][:, k * 128:(k + 1) * 128], identity=c.idn[:]),
                          reads=[a.b_gtc[gk], c.b_idn], writes=[bb])
                dst = a.v[i_][:, cidx_ * 512:(cidx_ + 1) * 512].rearrange("p (k t) -> p k t", k=4)
                if (i_ + cidx_) % 2 == 0:
                    fw.op("act", lambda e: e.activation(out=dst, in_=pv, func=AF.Copy), reads=[bb], writes=[a.b_v[i_][cidx_]])
                else:
                    fw.op("dve", lambda e: e.tensor_copy(out=dst, in_=pv), reads=[bb], writes=[a.b_v[i_][cidx_]])

        for cidx in range(8):
            Wu, bWu = load_W(c, a, wb_in, wbufs_in, 0, cidx)
            Wz, bWz = load_W(c, a, wb_in, wbufs_in, 0, 16 + cidx)
            for i in range(4):
                pm, bpm = next_bank(c)
                for gg in range(2):
                    g = 2 * cidx + gg
                    fw.op("pe", lambda e: e.matmul(pm[:, gg * 256:(gg + 1) * 256], lhsT=a.wsT[:, g, :],
                                                   rhs=a.v[i][:, g * 256:(g + 1) * 256], start=True, stop=True),
                          reads=[a.b_wsT, a.b_v[i][cidx]], writes=[bpm])
                pu, bpu = next_bank(c)
                mm16(pu, bpu, lambda k: hT[:, k, i * 128:(i + 1) * 128], Wu, bWu, [b_hT])
                pz, bpz = next_bank(c)
                mm16(pz, bpz, lambda k: hT[:, k, i * 128:(i + 1) * 128], Wz, bWz, [b_hT])
                flush_tr()
                r = a.rr; a.rr = 1 - r
                t1, bt1, t2, bt2, t3, bt3 = a.t1[r], a.b_t1[r], a.t2[r], a.b_t2[r], a.t3[r], a.b_t3[r]
                cs = slice(cidx * 512, (cidx + 1) * 512)
                fw.op("act", lambda e: e.activation(out=t1[:], in_=pu[:, :], func=AF.Gelu), reads=[bpu], writes=[bt1])
                fw.op("act", lambda e: e.activation(out=t2[:], in_=pz[:, :], func=AF.Silu), reads=[bpz], writes=[bt2])
                fw.op("dve", lambda e: e.tensor_tensor(out=t3[:], in0=pm[:, :], in1=Bm[:, cs], op=ALU.add), reads=[bpm, b_Bm], writes=[bt3])
                fw.op("dve", lambda e: e.tensor_tensor(out=t1[:], in0=t1[:], in1=t2[:], op=ALU.mult), reads=[bt1, bt2], writes=[bt1])
                gk = a.gtr; a.gtr = (gk + 1) % 3
                fw.op("dve", lambda e: e.tensor_tensor(out=a.gtc[gk][:], in0=t1[:], in1=t3[:], op=ALU.mult), reads=[bt1, bt3], writes=[a.b_gtc[gk]])
                pend.append((i, cidx, gk))
            if gi + 1 < ngroups:
                tn = (gi + 1) * 4 + cidx // 2
                if cidx % 2 == 0:
                    rms_pre(c, a, x_in[tn * 128:(tn + 1) * 128, :], c.bX[tn], a.g_bc, a.b_gbc)
                else:
                    rms_tr(c, a, cidx // 2, a.hTs[(gi + 1) % 2], a.b_hTs[(gi + 1) % 2])
        flush_tr()
        for nq in range(4):
            pys = [next_bank(c) for _ in range(4)]
            xslots = []
            for i in range(4):
                t = gi * 4 + i
                r = a.xrr; a.xrr = (r + 1) % 4
                xslots.append(r)
                fw.dma("act", a.xr[r][:], x_in[t * 128:(t + 1) * 128, nq * 512:(nq + 1) * 512], reads=[c.bX[t]], writes=[a.b_xr[r]], semkey="xr%d" % r)
            for kh in range(2):
                W, bW = load_W(c, a, wb_out, wbufs_out, kh, nq)
                for i in range(4):
                    py, bpy = pys[i]
                    mm16(py, bpy, lambda k: a.v[i][:, (16 * kh + k) * 128:(16 * kh + k + 1) * 128], W, bW,
                         lambda k: [a.b_v[i][(16 * kh + k) // 4]], start_first=(kh == 0), stop_last=(kh == 1))
            for i in range(4):
                t = gi * 4 + i
                py, bpy = pys[i]
                r = xslots[i]
                rows = slice(t * 128, (t + 1) * 128)
                cols = slice(nq * 512, (nq + 1) * 512)
                fw.op("dve", lambda e: e.tensor_tensor(out=a.xr[r][:], in0=py[:, :], in1=a.xr[r][:], op=ALU.add),
                      reads=[bpy, a.b_xr[r]], writes=[a.b_xr[r]])
                fw.dma("act", x_out[rows, cols], a.xr[r][:], reads=[a.b_xr[r]], writes=[c.bX[t]], semkey="xr%d" % r)


BW = 2048
DIL = (1, 4, 16)
SCALE = float(1.0 / np.sqrt(128.0))
BIG = 30000.0


def alibi_a(g, h):
    n = np.float32(g * 16 + h + 1)
    slope = np.power(np.float32(2.0), np.float32(-8.0) * n / np.float32(48))
    return float(np.float32(slope) * np.float32(DIL[g]))


def convert_B_in(c, w_ap, name):
    nc = c.nc
    wb = nc.dram_tensor(name, [16, 128, 16 * 10 * 128], BF16).ap()
    bufs = []
    end = 0
    for h in range(16):
        b = Buf("%s_%d" % (name, h))
        dst_h = wb[h].rearrange("p (k m n) -> p k m n", k=16, m=10)
        for m in range(10):
            src = w_ap[:, m * 2048 + h * 128: m * 2048 + (h + 1) * 128].rearrange("(k p) n -> p k n", p=128)
            end = c.pump.add(dst_h[:, :, m, :], src, b)
        bufs.append(b)
    return wb, bufs, end


def phase_B0(c, es, j, P, x_in, hTd, b_hTd):
    fw = c.fw
    sb = lambda name, shape, dt: es.enter_context(c.nc.sbuf_tensor(uniq(name), shape, dt))
    a = Ctx()
    a.xt = sb("b0_xt", [128, D], F32); a.b_xt = Buf("b0_xt")
    a.hb = sb("b0_hb", [128, D], BF16); a.b_hb = Buf("b0_hb")
    a.st = sb("b0_st", [128, 16], F32); a.b_st = Buf("b0_st")
    gbc = sb("b0_gbc", [128, D], F32); b_gbc = Buf("b0_gbc")
    hT = [sb("b0_hT%d" % i, [128, 16, 512], BF16) for i in range(2)]
    b_hT = [Buf("b0_hT%d" % i) for i in range(2)]
    fw.dma("sp", gbc[:], P["b_norm_g"][j].partition_broadcast(128), writes=[b_gbc], semkey="gbc")
    for gi in range(c.NT // 4):
        s = gi % 2
        for i in range(4):
            t = gi * 4 + i
            rms_to_hT(c, a, x_in[t * 128:(t + 1) * 128, :], c.bX[t], i, hT[s], b_hT[s], gbc, b_gbc)
        fw.dma("sp", hTd[:, :, gi * 512:(gi + 1) * 512], hT[s][:], reads=[b_hT[s]], writes=[b_hTd[gi]], semkey="B0_hT%d" % s)


def phase_B1(c, es, j, wb, wbufs, hTd, b_hTd, ozTd, b_ozTd):
    fw = c.fw
    nc = c.nc
    T = c.T
    NSB = T // 2048
    sb_ = lambda name, shape, dt: es.enter_context(nc.sbuf_tensor(uniq(name), shape, dt))
    Wh = sb_("b1_Wh", [128, 16, 10, 128], BF16); b_Wh = Buf("b1_Wh")
    hTg = [sb_("b1_hTg%d" % i, [128, 16, 512], BF16) for i in range(2)]; b_hTg = [Buf("b1_hTg%d" % i) for i in range(2)]
    KT = [sb_("b1_KT%d" % g, [128, T], BF16) for g in range(3)]
    V = [sb_("b1_V%d" % g, [128, T // 128, 128], BF16) for g in range(3)]
    QT = [sb_("b1_QT%d" % g, [128, 2048], BF16) for g in range(3)]
    sz = sb_("b1_sz", [128, 2048], F32)
    VT2 = sb_("b1_VT2", [128, 2048], BF16); b_VT2 = Buf("b1_VT2")
    VTt = [sb_("b1_VTt%d" % i, [128, 512], BF16) for i in range(2)]; b_VTt = [Buf("b1_VTt%d" % i) for i in range(2)]
    accs = [sb_("b1_acc%d" % i, [128, 2, 2048], F32) for i in range(2)]; b_accs = [Buf("b1_acc%d" % i) for i in range(2)]
    Et = [sb_("b1_E%d" % i, [128, 256], BF16) for i in range(5)]; b_E = [Buf("b1_E%d" % i) for i in range(5)]
    tmp = [sb_("b1_tmp%d" % i, [128, 256], F32) for i in range(5)]; b_tmp = [Buf("b1_tmp%d" % i) for i in range(5)]
    distf = sb_("b1_dist", [128, 256], F32); b_dist = Buf("b1_dist")
    ones = sb_("b1_ones", [128, 128], BF16); b_ones = Buf("b1_ones")
    ob = [sb_("b1_ob%d" % i, [128, 2048], BF16) for i in range(2)]; b_ob = [Buf("b1_ob%d" % i) for i in range(2)]
    fw.op("pool", lambda e: e.memset(ones[:], 1.0), writes=[b_ones])
    fw.op("pool", lambda e: e.iota(distf[:, 0:128], pattern=[[1, 128]], base=128, channel_multiplier=-1, allow_small_or_imprecise_dtypes=True), writes=[b_dist])
    fw.op("pool", lambda e: e.iota(distf[:, 128:256], pattern=[[1, 128]], base=0, channel_multiplier=-1, allow_small_or_imprecise_dtypes=True), writes=[b_dist])
    fw.op("pool", lambda e: e.affine_select(out=distf[:, 0:128], in_=distf[:, 0:128], pattern=[[-1, 128]], compare_op=ALU.is_ge,
                                            fill=BIG, base=0, channel_multiplier=1), reads=[b_dist], writes=[b_dist])
    fw.op("pool", lambda e: e.affine_select(out=distf[:, 128:256], in_=distf[:, 128:256], pattern=[[1, 128]], compare_op=ALU.is_ge,
                                            fill=BIG, base=0, channel_multiplier=-1), reads=[b_dist], writes=[b_dist])
    rr = {"hTg": 0, "vt": 0, "e": 0, "ob": 0, "ev": 0, "acc": 0}
    NG = T // 512
    b_KT = [[Buf("b1_KT%d_%d" % (g, i)) for i in range(NG)] for g in range(3)]
    b_V = [[Buf("b1_V%d_%d" % (g, i)) for i in range(NG)] for g in range(3)]
    b_QT = [[Buf("b1_QT%d_%d" % (g, i)) for i in range(4)] for g in range(3)]
    b_szg = [Buf("b1_sz_%d" % i) for i in range(4)]

    def evac(dst, src_bank, bb, wbuf, func=None):
        if func is not None:
            fw.op("act", lambda e: e.activation(out=dst, in_=src_bank, func=func), reads=[bb], writes=[wbuf])
            return
        rr["ev"] ^= 1
        if rr["ev"]:
            fw.op("act", lambda e: e.activation(out=dst, in_=src_bank, func=AF.Copy), reads=[bb], writes=[wbuf])
        else:
            fw.op("dve", lambda e: e.tensor_copy(out=dst, in_=src_bank), reads=[bb], writes=[wbuf])

    def v_transposes(g, src, b_src, cols_of, blks, wbuf):
        n = len(blks)
        for q0 in range(0, n, 8):
            bank, bb = next_bank(c)
            pv = bank[:].bitcast(BF16).rearrange("p (k t) -> p k t", k=8)
            m = min(8, n - q0)
            for k in range(m):
                fw.op("pe", lambda e: e.transpose(out=pv[:, k, :], in_=cols_of(q0 + k), identity=c.idn[:]),
                      reads=[b_src, c.b_idn], writes=[bb])
            evac(V[g][:, blks[q0]:blks[q0] + m, :], pv[:, 0:m, :], bb, wbuf)

    LAG = 3
    NE = len(Et)
    deferred = []

    def flush_deferred():
        while deferred:
            h_, sbi_, os__ = deferred.pop(0)
            fw.dma("sp", ozTd[h_, :, sbi_ * 2048:(sbi_ + 1) * 2048], ob[os__][:], reads=[b_ob[os__]], writes=[b_ozTd[h_]],
                   semkey="B1_ob%d" % os__)

    for h in range(16):
        fw.dma("sp", Wh[:].rearrange("p k m n -> p (k m n)"), wb[h], reads=[wbufs[h]], writes=[b_Wh], semkey="B1_Wh")
        for sbi in range(NSB):
            ai = rr["acc"]; rr["acc"] ^= 1
            acc, b_acc = accs[ai], b_accs[ai]
            items = []
            st = {"i1": 0, "i2": 0}
            st1 = {}

            def item_info(g, jl, r):
                d = DIL[g]
                span = 128 * d
                nj = 2048 // span
                jg = sbi * nj + jl
                if g == 0:
                    gl_ = jl // 4
                    qb = [b_QT[0][gl_]]
                    kc = [b_KT[0][jg // 4]]; kp = [b_KT[0][(jg - 1) // 4]] if jg > 0 else []
                    vc = [b_V[0][jg // 4]]; vp = [b_V[0][(jg - 1) // 4]] if jg > 0 else []
                elif g == 1:
                    qb = [b_QT[1][jl]]
                    kc = [b_KT[1][jg]]; kp = [b_KT[1][jg - 1]] if jg > 0 else []
                    vc = [b_V[1][jg]]; vp = [b_V[1][jg - 1]] if jg > 0 else []
                else:
                    qb = list(b_QT[2])
                    kc = [b_KT[2][sbi * 4 + i] for i in range(4)]
                    kp = [b_KT[2][(sbi - 1) * 4 + i] for i in range(4)] if jg > 0 else []
                    vc = [b_V[2][sbi]]; vp = [b_V[2][sbi - 1]] if jg > 0 else []
                return d, span, jg, qb, kc, kp, vc, vp

            def stage1(idx):
                g, jl, r = items[idx]
                d, span, jg, qb, kc, kp, vc, vp = item_info(g, jl, r)
                coef = -alibi_a(g, h) / SCALE
                qcols = QT[g][:, jl * span + r: (jl + 1) * span: d]
                kcur = KT[g][:, jg * span + r: (jg + 1) * span: d]
                has_prev = jg > 0
                ps, bps = next_bank(c)
                if has_prev:
                    kprev = KT[g][:, (jg - 1) * span + r: jg * span: d]
                    fw.op("pe", lambda e: e.matmul(ps[:, 0:128], lhsT=kprev, rhs=qcols, start=True, stop=True),
                          reads=kp + qb, writes=[bps])
                fw.op("pe", lambda e: e.matmul(ps[:, 128:256], lhsT=kcur, rhs=qcols, start=True, stop=True),
                      reads=kc + qb, writes=[bps])
                es_ = rr["e"]; rr["e"] = (es_ + 1) % NE
                cs = slice(0, 256) if has_prev else slice(128, 256)
                fw.op("dve", lambda e: e.scalar_tensor_tensor(out=tmp[es_][:, cs], in0=distf[:, cs], scalar=coef, in1=ps[:, cs],
                                                              op0=ALU.mult, op1=ALU.add),
                      reads=[b_dist, bps], writes=[b_tmp[es_]])
                fw.op("act", lambda e: e.activation(out=Et[es_][:, cs], in_=tmp[es_][:, cs], func=AF.Exp, scale=SCALE),
                      reads=[b_tmp[es_]], writes=[b_E[es_]])
                st1[idx] = es_

            def stage2(idx):
                g, jl, r = items[idx]
                d, span, jg, qb, kc, kp, vc, vp = item_info(g, jl, r)
                has_prev = jg > 0
                es_ = st1.pop(idx)
                blk_cur = jg * d + r
                po, bpo = next_bank(c)
                first = True
                for is_ones in (False, True):
                    oc = slice(128, 256) if is_ones else slice(0, 128)
                    if has_prev:
                        l0 = ones[:] if is_ones else V[g][:, blk_cur - d, :]
                        fw.op("pe", lambda e: e.matmul(po[:, oc], lhsT=l0, rhs=Et[es_][:, 0:128], start=first, stop=False,
                                                       skip_group_check=True),
                              reads=([b_ones] if is_ones else vp) + [b_E[es_]], writes=[bpo])
                        first = False
                    l1 = ones[:] if is_ones else V[g][:, blk_cur, :]
                    fw.op("pe", lambda e: e.matmul(po[:, oc], lhsT=l1, rhs=Et[es_][:, 128:256], start=first, stop=True,
                                                   skip_group_check=True),
                          reads=([b_ones] if is_ones else vc) + [b_E[es_]], writes=[bpo])
                    first = False
                pov = po[:, 0:256].rearrange("p (a q) -> p a q", a=2)
                dst = acc[:, :, jl * span + r: (jl + 1) * span: d]
                if g == 0:
                    fw.op("act", lambda e: e.activation(out=dst, in_=pov, func=AF.Copy), reads=[bpo], writes=[b_acc])
                else:
                    fw.op("dve", lambda e: e.tensor_tensor(out=dst, in0=pov, in1=dst, op=ALU.add), reads=[bpo, b_acc], writes=[b_acc])

            def step(drain=False):
                did = False
                if st["i1"] < len(items):
                    stage1(st["i1"]); st["i1"] += 1
                    did = True
                if st["i2"] < st["i1"] and (st["i1"] - st["i2"] > LAG or (drain and st["i1"] == len(items))):
                    stage2(st["i2"]); st["i2"] += 1
                    did = True
                return did

            for gl in range(4):
                gidx = sbi * 4 + gl
                s = rr["hTg"]; rr["hTg"] ^= 1
                fw.dma("sp", hTg[s][:], hTd[:, :, gidx * 512:(gidx + 1) * 512], reads=[b_hTd[gidx]], writes=[b_hTg[s]],
                       semkey="B1_hTg%d" % s)
                if gl == 2:
                    flush_deferred()
                lc = slice(gl * 512, (gl + 1) * 512)
                gc = slice(gidx * 512, (gidx + 1) * 512)
                for m in range(10):
                    bank, bb = next_bank(c)
                    for k in range(16):
                        fw.op("pe", lambda e: e.matmul(bank[:, :], lhsT=Wh[:, k, m, :], rhs=hTg[s][:, k, :], start=(k == 0), stop=(k == 15)),
                              reads=[b_Wh, b_hTg[s]], writes=[bb])
                    g, kind = divmod(m, 3)
                    if m == 9:
                        evac(sz[:, lc], bank[:, :], bb, b_szg[gl], func=AF.Silu)
                    elif kind == 0:
                        evac(QT[g][:, lc], bank[:, :], bb, b_QT[g][gl])
                    elif kind == 1:
                        evac(KT[g][:, gc], bank[:, :], bb, b_KT[g][gidx])
                    else:
                        if g == 2:
                            evac(VT2[:, lc], bank[:, :], bb, b_VT2)
                        else:
                            vs = rr["vt"]; rr["vt"] ^= 1
                            evac(VTt[vs][:], bank[:, :], bb, b_VTt[vs])
                            if g == 0:
                                v_transposes(0, VTt[vs], b_VTt[vs], lambda i: VTt[vs][:, i * 128:(i + 1) * 128],
                                             [gidx * 4 + i for i in range(4)], b_V[0][gidx])
                            else:
                                v_transposes(1, VTt[vs], b_VTt[vs], lambda r: VTt[vs][:, r:512:4],
                                             [gidx * 4 + r for r in range(4)], b_V[1][gidx])
                    step()
                for i in range(4):
                    items.append((0, gl * 4 + i, 0))
                for r in range(4):
                    items.append((1, gl, r))
            v_transposes(2, VT2, b_VT2, lambda r: VT2[:, r:2048:16], [sbi * 16 + r for r in range(16)], b_V[2][sbi])
            for r in range(16):
                items.append((2, 0, r))
            while step(drain=True):
                pass
            assert st["i2"] == len(items) and not st1
            c.pump.pump(c.pump_rate)
            fw.op("dve", lambda e: e.reciprocal(out=acc[:, 1, :], in_=acc[:, 1, :]), reads=[b_acc], writes=[b_acc])
            fw.op("dve", lambda e: e.tensor_tensor(out=acc[:, 0, :], in0=acc[:, 0, :], in1=acc[:, 1, :], op=ALU.mult), reads=[b_acc], writes=[b_acc])
            os_ = rr["ob"]; rr["ob"] ^= 1
            fw.op("dve", lambda e: e.tensor_tensor(out=ob[os_][:], in0=acc[:, 0, :], in1=sz[:], op=ALU.mult), reads=[b_acc] + b_szg, writes=[b_ob[os_]])
            deferred.append((h, sbi, os_))
    flush_deferred()


def phase_B3(c, es, wbo, wbufs_o, ozTd, b_ozTd, x_in, x_out):
    fw = c.fw
    nc = c.nc
    sb_ = lambda name, shape, dt: es.enter_context(nc.sbuf_tensor(uniq(name), shape, dt))
    ozg = [sb_("b3_ozg%d" % i, [128, 16, 512], BF16) for i in range(2)]; b_ozg = [Buf("b3_ozg%d" % i) for i in range(2)]
    W = [sb_("b3_W%d" % i, [128, 16, 512], BF16) for i in range(4)]; b_W = [Buf("b3_W%d" % i) for i in range(4)]
    xr = [sb_("b3_xr%d" % i, [128, 2048], F32) for i in range(3)]; b_xr = [Buf("b3_xr%d" % i) for i in range(3)]
    for nq in range(4):
        fw.dma("sp", W[nq][:].rearrange("p k n -> p (k n)"), wbo[0, nq], reads=[wbufs_o[(0, nq)]], writes=[b_W[nq]], semkey="A_W%d" % (nq % 3) if nq < 3 else "B3_W3")
    rr = 0
    for gi in range(c.NT // 4):
        s = gi % 2
        fw.dma("sp", ozg[s][:], ozTd[:, :, gi * 512:(gi + 1) * 512].rearrange("h p t -> p h t"), reads=b_ozTd, writes=[b_ozg[s]], semkey="B3_ozg%d" % s)
        for i in range(4):
            t = gi * 4 + i
            r = rr; rr = (rr + 1) % 3
            rows = slice(t * 128, (t + 1) * 128)
            fw.dma("act", xr[r][:], x_in[rows, :], reads=[c.bX[t]], writes=[b_xr[r]], semkey="xr%d" % r)
            for nq in range(4):
                bank, bb = next_bank(c)
                for k in range(16):
                    fw.op("pe", lambda e: e.matmul(bank[:, :], lhsT=ozg[s][:, k, i * 128:(i + 1) * 128], rhs=W[nq][:, k, :],
                                                   start=(k == 0), stop=(k == 15)), reads=[b_ozg[s], b_W[nq]], writes=[bb])
                cols = slice(nq * 512, (nq + 1) * 512)
                fw.op("dve", lambda e: e.tensor_tensor(out=xr[r][:, cols], in0=bank[:, :], in1=xr[r][:, cols], op=ALU.add), reads=[bb, b_xr[r]], writes=[b_xr[r]])
            fw.dma("act", x_out[rows, :], xr[r][:], reads=[b_xr[r]], writes=[c.bX[t]], semkey="xr%d" % r)


def phase_final(c, es, P, x_in, x_out):
    fw = c.fw
    sb_ = lambda name, shape, dt: es.enter_context(c.nc.sbuf_tensor(uniq(name), shape, dt))
    xt = [sb_("f_xt%d" % i, [128, D], F32) for i in range(2)]; b_xt = [Buf("f_xt%d" % i) for i in range(2)]
    jk = sb_("f_jk", [128, D], BF16); b_jk = Buf("f_jk")
    st = sb_("f_st", [128, 4], F32); b_st = Buf("f_st")
    gbc = sb_("f_gbc", [128, D], F32); b_gbc = Buf("f_gbc")
    fw.dma("sp", gbc[:], P["final_norm_g"].partition_broadcast(128), writes=[b_gbc], semkey="gbc")
    for t in range(c.NT):
        s = t % 2
        rows = slice(t * 128, (t + 1) * 128)
        fw.dma("sp", xt[s][:], x_in[rows, :], reads=[c.bX[t]], writes=[b_xt[s]], semkey="xr%d" % s)
        fw.op("act", lambda e: e.activation(out=jk[:], in_=xt[s][:], func=AF.Square, scale=float(D ** -0.5), accum_out=st[:, 0:1]),
              reads=[b_xt[s]], writes=[b_jk, b_st])
        fw.op("act", lambda e: e.activation(out=st[:, 1:2], in_=st[:, 0:1], func=AF.Sqrt, bias=EPS, scale=1.0), reads=[b_st], writes=[b_st])
        fw.op("dve", lambda e: e.reciprocal(out=st[:, 2:3], in_=st[:, 1:2]), reads=[b_st], writes=[b_st])
        fw.op("dve", lambda e: e.scalar_tensor_tensor(out=xt[s][:], in0=xt[s][:], scalar=st[:, 2:3], in1=gbc[:], op0=ALU.mult, op1=ALU.mult),
              reads=[b_xt[s], b_st, b_gbc], writes=[b_xt[s]])
        fw.dma("sp", x_out[rows, :], xt[s][:], reads=[b_xt[s]], writes=[c.bX[t]], semkey="xr%d" % s)


T_CORE = 4096
N_CORES = 4
from concourse.bass_utils import run_bass_kernel_spmd


def build_program(n_pairs=2, final=True):
    T = T_CORE
    nc = bass.Bass("TRN2", target_bir_lowering=False)
    P = {}

    def inp(name, shape):
        P[name] = nc.dram_tensor(name, shape, F32, kind="ExternalInput").ap()
    inp("x", [T, 2048])
    inp("a_norm_g", [2, 2048]); inp("a_w_in", [2, 2048, 12288]); inp("a_ln_g", [2, 4096]); inp("a_ln_b", [2, 4096])
    inp("a_w_s", [2, 16, 128, 128]); inp("a_b_s", [2, 16, 128]); inp("a_w_out", [2, 4096, 2048])
    inp("b_norm_g", [2, 2048]); inp("b_w_in", [2, 2048, 20480]); inp("b_w_out", [2, 2048, 2048]); inp("final_norm_g", [2048])
    out = nc.dram_tensor("out", [T, 2048], F32, kind="ExternalOutput").ap()
    hTd = nc.dram_tensor("hTd", [128, 16, T], BF16).ap()
    ozTd = nc.dram_tensor("ozTd", [16, 128, T], BF16).ap()
    es = ExitStack()
    with es:
        fw = FW(nc, es)
        c = setup_common(nc, es, fw, T)
        conv = []
        a_order_in = [(0, 8 + i) for i in range(8)]
        for i in range(8):
            a_order_in += [(0, i), (0, 16 + i)]
        a_order_out = [(kh, nq) for nq in range(4) for kh in range(2)]
        for j in range(2):
            wa_in = convert_weights(c, P["a_w_in"][j], 2048, 12288, "wb_a_in%d" % j, order=a_order_in)
            wa_out = convert_weights(c, P["a_w_out"][j], 4096, 2048, "wb_a_out%d" % j, order=a_order_out)
            wb_in = convert_B_in(c, P["b_w_in"][j], "wb_b_in%d" % j)
            wb_out = convert_weights(c, P["b_w_out"][j], 2048, 2048, "wb_b_out%d" % j)
            conv.append((wa_in, wa_out, wb_in, wb_out))
        x_cur = P["x"]
        for j in range(n_pairs):
            wa_in, wa_out, wb_in, wb_out = conv[j]
            c.pump.ensure(wa_out[2])
            c.pump_rate = 22 if j == 0 else 0
            with ExitStack() as es2:
                a = setup_A(c, es2)
                phase_A(c, a, j, P, x_cur, out, wa_in[0], wa_in[1], wa_out[0], wa_out[1])
                fw.barrier()
            x_cur = out
            c.pump.ensure(wb_out[2])
            c.pump_rate = 7 if j == 0 else 0
            b_hTd = [Buf("hTd%d" % i) for i in range(T // 512)]
            b_ozTd = [Buf("ozTd%d" % i) for i in range(16)]
            with ExitStack() as es2:
                phase_B0(c, es2, j, P, x_cur, hTd, b_hTd)
                fw.barrier()
            with ExitStack() as es2:
                phase_B1(c, es2, j, wb_in[0], wb_in[1], hTd, b_hTd, ozTd, b_ozTd)
                fw.barrier()
            with ExitStack() as es2:
                phase_B3(c, es2, wb_out[0], wb_out[1], ozTd, b_ozTd, x_cur, out)
                fw.barrier()
        if n_pairs == 2:
            c.pump.ensure(len(c.pump.q))
        with ExitStack() as es2:
            phase_final(c, es2, P, out, out)
            fw.barrier()
        fw.finish(c.bX, e="sp")
        fw.finish(c.bX, e="act")
    return nc


def kernel(**inputs):
    x = np.ascontiguousarray(np.asarray(inputs["x"], dtype=np.float32))
    nc = build_program()
    shared = {k: np.ascontiguousarray(np.asarray(v, dtype=np.float32)) for k, v in inputs.items() if k != "x"}
    in_maps = []
    for b in range(N_CORES):
        m = dict(shared)
        m["x"] = x[b]
        in_maps.append(m)
    res = run_bass_kernel_spmd(nc, in_maps, core_ids=list(range(N_CORES)))
    return np.stack([np.asarray(res.results[b]["out"], dtype=np.float32) for b in range(N_CORES)], axis=0)
```

```python
import numpy as np
import concourse.bass as bass
import concourse.mybir as mybir
from contextlib import ExitStack
F32 = mybir.dt.float32; BF16 = mybir.dt.bfloat16
AF = mybir.ActivationFunctionType
ALU = mybir.AluOpType
AX = mybir.AxisListType


_UID = [0]


def uniq(name):
    _UID[0] += 1
    return "%s_u%d" % (name, _UID[0])


class Buf:
    __slots__ = ("name", "w", "r")

    def __init__(self, name):
        self.name = name
        self.w = None
        self.r = {}


class FW:
    def __init__(self, nc, es, n_dma_sems=50):
        self.nc = nc
        self.es = es
        self.eng = {"pe": nc.tensor, "act": nc.scalar, "dve": nc.vector, "pool": nc.gpsimd, "sp": nc.sync}
        self.sem = {}
        self.cnt = {}
        for k in ("pe", "act", "dve", "pool"):
            self.sem[k] = es.enter_context(nc.semaphore("sem_" + k))
            self.cnt[k] = 0
        self.n_dma = 0
        self.waited = {k: {} for k in self.eng}
        self.dma_free = []
        self.n_dma_sems = n_dma_sems
        self.dma_rr = 0
        self.q_rr = {}
        self.keymap = {}
        self.ninst = 0

    def _need(self, e, deps):
        need = {}
        for d in deps:
            if d is None:
                continue
            k, v = d
            if need.get(k, 0) < v:
                need[k] = v
        w = self.waited[e]
        eng = self.eng[e]
        for k, v in need.items():
            if k == e and e == "pe":
                continue
            if w.get(k, 0) >= v:
                continue
            eng.wait_ge(self.sem[k], v)
            w[k] = v
            self.ninst += 1

    def _deps(self, reads, writes):
        deps = []
        for b in reads:
            deps.append(b.w)
        for b in writes:
            deps.append(b.w)
            for k, v in b.r.items():
                deps.append((k, v))
        return deps

    def _mark(self, ev, reads, writes):
        k, v = ev
        for b in reads:
            b.r[k] = v
        for b in writes:
            b.w = ev
            b.r = {}

    def op(self, e, fn, reads=(), writes=()):
        self._need(e, self._deps(reads, writes))
        ins = fn(self.eng[e])
        self.cnt[e] += 1
        ins.then_inc(self.sem[e], 1)
        self._mark((e, self.cnt[e]), reads, writes)
        self.ninst += 1
        return ins

    def dma(self, q, out, in_, reads=(), writes=(), semkey=None):
        self._need(q, self._deps(reads, writes))
        if semkey is None:
            raise ValueError("dma needs a semkey (one in-flight DMA per key)")
        if semkey not in self.keymap:
            assert len(self.keymap) < self.n_dma_sems, "out of DMA semaphores"
            k = "dma%d" % len(self.keymap)
            self.keymap[semkey] = k
            self.sem[k] = self.es.enter_context(self.nc.semaphore("sem_" + k))
            self.cnt[k] = 0
        semkey = self.keymap[semkey]
        ins = self.eng[q].dma_start(out=out, in_=in_)
        self.cnt[semkey] += 16
        ins.then_inc(self.sem[semkey], 16)
        self._mark((semkey, self.cnt[semkey]), reads, writes)
        self.ninst += 1
        return ins

    def finish(self, bufs, e="sp"):
        self._need(e, [b.w for b in bufs])


def fw_barrier(fw):
    evs = [(k, v) for k, v in fw.cnt.items() if v > 0]
    for e in ("pe", "act", "dve", "pool", "sp"):
        w = fw.waited[e]
        for k, v in evs:
            if w.get(k, 0) >= v:
                continue
            fw.eng[e].wait_ge(fw.sem[k], v)
            w[k] = v
FW.barrier = fw_barrier


D = 2048
AW = 4096
EPS = 1e-6
NCHUNK_IN_A = 24


class Ctx:
    pass


def setup_common(nc, es, fw, T):
    c = Ctx()
    c.nc, c.es, c.fw, c.T = nc, es, fw, T
    c.NT = T // 128
    sb = lambda name, shape, dt: es.enter_context(nc.sbuf_tensor(uniq(name), shape, dt))
    c.sb = sb
    c.banks = [es.enter_context(nc.psum_tensor("bank%d" % i, [128, 512], F32)) for i in range(8)]
    c.bbuf = [Buf("bank%d" % i) for i in range(8)]
    c.bank_rr = 0
    c.idf = sb("idf", [128, 128], F32); c.b_idf = Buf("idf")
    c.idn = sb("idn", [128, 128], BF16); c.b_idn = Buf("idn")
    fw.op("pool", lambda e: e.memset(c.idf[:], 1.0), writes=[c.b_idf])
    fw.op("pool", lambda e: e.affine_select(out=c.idf[:], in_=c.idf[:], pattern=[[-1, 128]], compare_op=ALU.is_equal,
                                            fill=0.0, base=0, channel_multiplier=1), reads=[c.b_idf], writes=[c.b_idf])
    fw.op("dve", lambda e: e.tensor_copy(out=c.idn[:], in_=c.idf[:]), reads=[c.b_idf], writes=[c.b_idn])
    c.bX = [Buf("X%d" % i) for i in range(c.NT)]
    c.pump = ConvPump(c)
    c.pump_rate = 0
    return c


def next_bank(c):
    i = c.bank_rr
    c.bank_rr = (i + 1) % 8
    return c.banks[i], c.bbuf[i]


class ConvPump:
    def __init__(self, c):
        self.c = c
        self.q = []
        self.pos = 0
        self.hist = []

    def add(self, dst, src, buf):
        self.q.append((dst, src, buf))
        return len(self.q)

    def pump(self, n, paced=True):
        fw = self.c.fw
        if paced and n > 0 and self.pos < len(self.q) and fw.cnt["pe"] > 0:
            fw._need("pool", [("pe", fw.cnt["pe"])])
        while n > 0 and self.pos < len(self.q):
            dst, src, buf = self.q[self.pos]
            self.pos += 1
            n -= 1
            if len(self.hist) >= 2:
                fw._need("pool", [self.hist[-2]])
            fw.dma("pool", dst, src, writes=[buf], semkey="conv%d" % (self.pos % 3))
            self.hist.append(buf.w)

    def ensure(self, upto):
        if self.pos < upto:
            self.pump(upto - self.pos, paced=False)


def convert_weights(c, w_ap, rows, cols, name, order=None):
    nc = c.nc
    nkh = rows // 2048
    ncc = cols // 512
    wb = nc.dram_tensor(name, [nkh, ncc, 128, 16 * 512], BF16).ap()
    bufs = {}
    if order is None:
        order = [(kh, cc) for kh in range(nkh) for cc in range(ncc)]
    end = 0
    for (kh, cc) in order:
        b = Buf("%s_%d_%d" % (name, kh, cc))
        src = w_ap[kh * 2048:(kh + 1) * 2048, cc * 512:(cc + 1) * 512].rearrange("(k p) n -> p k n", p=128)
        dst = wb[kh, cc].rearrange("p (k n) -> p k n", k=16)
        end = c.pump.add(dst, src, b)
        bufs[(kh, cc)] = b
    return wb, bufs, end


def setup_A(c, es):
    sb = lambda name, shape, dt: es.enter_context(c.nc.sbuf_tensor(uniq(name), shape, dt))
    a = Ctx()
    a.G = 4
    a.xt = sb("a_xt", [128, D], F32); a.b_xt = Buf("a_xt")
    a.hb = sb("a_hb", [128, D], BF16); a.b_hb = Buf("a_hb")
    a.st = sb("a_st", [128, 16], F32); a.b_st = Buf("a_st")
    a.hTs = [sb("a_hT%d" % i, [128, 16, 512], BF16) for i in range(2)]; a.b_hTs = [Buf("a_hT%d" % i) for i in range(2)]
    a.g_bc = sb("a_gbc", [128, D], F32); a.b_gbc = Buf("a_gbc")
    a.lng = sb("a_lng", [128, AW], F32); a.b_lng = Buf("a_lng")
    a.lnb = sb("a_lnb", [128, AW], F32); a.b_lnb = Buf("a_lnb")
    a.wsf = a.xt[:].rearrange("p (g s) -> p g s", g=16); a.b_wsf = a.b_xt
    a.wsb = a.hb[:].rearrange("p (g s) -> p g s", g=16); a.b_wsb = a.b_hb
    a.wsT = sb("a_wsT", [128, 16, 128], BF16); a.b_wsT = Buf("a_wsT")
    a.bs = sb("a_bs", [128, 16], F32); a.b_bs = Buf("a_bs")
    a.W = [sb("a_W%d" % i, [128, 16, 512], BF16) for i in range(3)]
    a.b_W = [Buf("a_W%d" % i) for i in range(3)]
    a.w_rr = 0
    a.v = [sb("a_v%d" % i, [128, AW], BF16) for i in range(4)]; a.b_v = [[Buf("a_v%d_%d" % (i, q)) for q in range(8)] for i in range(4)]
    a.gtc = [sb("a_gtc%d" % i, [128, 512], BF16) for i in range(3)]; a.b_gtc = [Buf("a_gtc%d" % i) for i in range(3)]
    a.gtr = 0
    a.t3 = [sb("a_t3_%d" % i, [128, 512], F32) for i in range(2)]; a.b_t3 = [Buf("a_t3_%d" % i) for i in range(2)]
    a.w1 = sb("a_w1", [128, 16], F32); a.b_w1 = Buf("a_w1")
    a.stats = [sb("a_stats%d" % i, [128, 8, 6], F32) for i in range(4)]; a.b_stats = [Buf("a_stats%d" % i) for i in range(4)]
    a.tmp32s = [sb("a_tmp32_%d" % i, [128, 512], F32) for i in range(3)]; a.b_tmp32s = [Buf("a_tmp32_%d" % i) for i in range(3)]
    a.tmr = 0
    a.mvs = [sb("a_mv%d" % i, [128, 8], F32) for i in range(4)]; a.b_mvs = [Buf("a_mv%d" % i) for i in range(4)]
    a.t1 = [sb("a_t1_%d" % i, [128, 512], F32) for i in range(2)]; a.b_t1 = [Buf("a_t1_%d" % i) for i in range(2)]
    a.t2 = [sb("a_t2_%d" % i, [128, 512], F32) for i in range(2)]; a.b_t2 = [Buf("a_t2_%d" % i) for i in range(2)]
    a.xr = [sb("a_xr%d" % i, [128, 512], F32) for i in range(4)]; a.b_xr = [Buf("a_xr%d" % i) for i in range(4)]
    a.xrr = 0
    a.rr = 0
    return a


def load_W(c, a, wb, wbufs, kh, cc):
    fw = c.fw
    i = a.w_rr
    a.w_rr = (i + 1) % 3
    fw.dma("sp", a.W[i][:].rearrange("p k n -> p (k n)"), wb[kh, cc], reads=[wbufs[(kh, cc)]], writes=[a.b_W[i]], semkey="A_W%d" % i)
    return a.W[i], a.b_W[i]


def rms_pre(c, a, x_src, bX, gbc, b_gbc, semkey="xt"):
    fw = c.fw
    fw.dma("sp", a.xt[:], x_src, reads=[bX], writes=[a.b_xt], semkey=semkey)
    fw.op("act", lambda e: e.activation(out=a.hb[:], in_=a.xt[:], func=AF.Square, scale=float(D ** -0.5),
                                        accum_out=a.st[:, 0:1]), reads=[a.b_xt], writes=[a.b_hb, a.b_st])
    fw.op("act", lambda e: e.activation(out=a.st[:, 1:2], in_=a.st[:, 0:1], func=AF.Sqrt, bias=EPS, scale=1.0),
          reads=[a.b_st], writes=[a.b_st])
    fw.op("dve", lambda e: e.reciprocal(out=a.st[:, 2:3], in_=a.st[:, 1:2]), reads=[a.b_st], writes=[a.b_st])
    fw.op("dve", lambda e: e.scalar_tensor_tensor(out=a.hb[:], in0=a.xt[:], scalar=a.st[:, 2:3], in1=gbc[:],
                                                  op0=ALU.mult, op1=ALU.mult),
          reads=[a.b_xt, a.b_st, b_gbc], writes=[a.b_hb])


def rms_tr(c, a, tile_in_group, hT, b_hT):
    fw = c.fw
    t0 = tile_in_group * 128
    for half in range(2):
        bank, bb = next_bank(c)
        pv = bank[:].bitcast(BF16).rearrange("p (k t) -> p k t", k=8)
        for k in range(8):
            kk = half * 8 + k
            fw.op("pe", lambda e: e.transpose(out=pv[:, k, :], in_=a.hb[:, kk * 128:(kk + 1) * 128], identity=c.idn[:]),
                  reads=[a.b_hb, c.b_idn], writes=[bb])
        if half == 0:
            fw.op("act", lambda e: e.activation(out=hT[:, half * 8:(half + 1) * 8, t0:t0 + 128], in_=pv, func=AF.Copy),
                  reads=[bb], writes=[b_hT])
        else:
            fw.op("dve", lambda e: e.tensor_copy(out=hT[:, half * 8:(half + 1) * 8, t0:t0 + 128], in_=pv),
                  reads=[bb], writes=[b_hT])


def rms_to_hT(c, a, x_src, bX, tile_in_group, hT, b_hT, gbc, b_gbc, hw=512):
    rms_pre(c, a, x_src, bX, gbc, b_gbc)
    rms_tr(c, a, tile_in_group, hT, b_hT)


def layer_A_consts(c, a, j, P):
    fw = c.fw
    fw.dma("sp", a.g_bc[:], P["a_norm_g"][j].partition_broadcast(128), writes=[a.b_gbc], semkey="gbc")
    fw.dma("sp", a.lng[:], P["a_ln_g"][j].partition_broadcast(128), writes=[a.b_lng], semkey="lng")
    fw.dma("sp", a.lnb[:], P["a_ln_b"][j].partition_broadcast(128), writes=[a.b_lnb], semkey="lnb")
    fw.dma("sp", a.wsf, P["a_w_s"][j].rearrange("g t s -> t g s"), writes=[a.b_wsf], semkey="xt")
    with c.nc.allow_non_contiguous_dma(reason="tiny bias transpose"):
        fw.dma("sp", a.bs[:], P["a_b_s"][j].rearrange("g t -> t g"), writes=[a.b_bs], semkey="bs")
    fw.op("pool", lambda e: e.affine_select(out=a.wsf, in_=a.wsf, pattern=[[0, 16], [-1, 128]],
                                            compare_op=ALU.is_ge, fill=0.0, base=0, channel_multiplier=1),
          reads=[a.b_wsf], writes=[a.b_wsf])
    fw.op("dve", lambda e: e.tensor_copy(out=a.wsb, in_=a.wsf), reads=[a.b_wsf], writes=[a.b_wsb])
    fw.op("dve", lambda e: e.tensor_reduce(out=a.w1[:], in_=a.wsf, axis=AX.X, op=ALU.add), reads=[a.b_wsf], writes=[a.b_w1])
    for g in range(16):
        gs = slice(g * 256, (g + 1) * 256)
        fw.op("dve", lambda e: e.tensor_scalar(out=a.lnb[:, gs], in0=a.lnb[:, gs], scalar1=a.w1[:, g:g + 1], scalar2=a.bs[:, g:g + 1],
                                               op0=ALU.mult, op1=ALU.add), reads=[a.b_lnb, a.b_w1, a.b_bs], writes=[a.b_lnb])
    for half in range(2):
        bank, bb = next_bank(c)
        pv = bank[:].bitcast(BF16).rearrange("p (k t) -> p k t", k=8)
        for k in range(8):
            g = half * 8 + k
            fw.op("pe", lambda e: e.transpose(out=pv[:, k, :], in_=a.wsb[:, g, :], identity=c.idn[:]),
                  reads=[a.b_wsb, c.b_idn], writes=[bb])
        fw.op("dve", lambda e: e.tensor_copy(out=a.wsT[:, half * 8:(half + 1) * 8, :], in_=pv), reads=[bb], writes=[a.b_wsT])


def phase_A(c, a, j, P, x_in, x_out, wb_in, wbufs_in, wb_out, wbufs_out):
    fw = c.fw
    NT = c.NT
    ngroups = NT // 4
    layer_A_consts(c, a, j, P)
    Bm, b_Bm = a.lnb, a.b_lnb

    def rms_tile(gi, i):
        t = gi * 4 + i
        rms_to_hT(c, a, x_in[t * 128:(t + 1) * 128, :], c.bX[t], i, a.hTs[gi % 2], a.b_hTs[gi % 2], a.g_bc, a.b_gbc)

    def mm16(bank, bb, lhs_of_k, W, bW, extra_reads, start_first=True, stop_last=True):
        for k in range(16):
            fw.op("pe", lambda e: e.matmul(bank[:, :], lhsT=lhs_of_k(k), rhs=W[:, k, :],
                                           start=(start_first and k == 0), stop=(stop_last and k == 15)),
                  reads=[bW] + (extra_reads(k) if callable(extra_reads) else extra_reads), writes=[bb])

    for i in range(4):
        rms_tile(0, i)
    for gi in range(ngroups):
        hT, b_hT = a.hTs[gi % 2], a.b_hTs[gi % 2]
        for cidx in range(8):
            W, bW = load_W(c, a, wb_in, wbufs_in, 0, 8 + cidx)
            for i in range(4):
                bank, bb = next_bank(c)
                mm16(bank, bb, lambda k: hT[:, k, i * 128:(i + 1) * 128], W, bW, [b_hT])
                vs = a.v[i][:, cidx * 512:(cidx + 1) * 512]
                fw.op("act", lambda e: e.activation(out=vs, in_=bank[:, :], func=AF.Gelu), reads=[bb], writes=[a.b_v[i][cidx]])
                fw.op("dve", lambda e: e.bn_stats(out=a.stats[i][:, cidx, :], in_=vs), reads=[a.b_v[i][cidx]], writes=[a.b_stats[i]])
        for i in range(4):
            mv, bmv = a.mvs[i], a.b_mvs[i]
            fw.op("dve", lambda e: e.bn_aggr(out=mv[:, 0:2], in_=a.stats[i][:]), reads=[a.b_stats[i]], writes=[bmv])
        for i in range(4):
            mv, bmv = a.mvs[i], a.b_mvs[i]
            fw.op("act", lambda e: e.activation(out=mv[:, 2:3], in_=mv[:, 1:2], func=AF.Sqrt, bias=EPS, scale=1.0),
                  reads=[bmv], writes=[bmv])
        for i in range(4):
            mv, bmv = a.mvs[i], a.b_mvs[i]
            fw.op("dve", lambda e: e.reciprocal(out=mv[:, 3:4], in_=mv[:, 2:3]), reads=[bmv], writes=[bmv])
        for i in range(4):
            mv, bmv = a.mvs[i], a.b_mvs[i]
            fw.op("dve", lambda e: e.tensor_scalar(out=mv[:, 4:5], in0=mv[:, 0:1], scalar1=-1.0, scalar2=mv[:, 3:4],
                                                   op0=ALU.mult, op1=ALU.mult), reads=[bmv], writes=[bmv])
        for i in range(4):
            mv, bmv = a.mvs[i], a.b_mvs[i]
            for q in range(8):
                sl = slice(q * 512, (q + 1) * 512)
                tm, btm = a.tmp32s[a.tmr], a.b_tmp32s[a.tmr]
                a.tmr = (a.tmr + 1) % 3
                fw.op("act", lambda e: e.activation(out=tm[:], in_=a.v[i][:, sl], func=AF.Identity,
                                                    bias=mv[:, 4:5], scale=mv[:, 3:4]),
                      reads=[a.b_v[i][q], bmv], writes=[btm])
                fw.op("dve", lambda e: e.tensor_tensor(out=a.v[i][:, sl], in0=tm[:], in1=a.lng[:, sl], op=ALU.mult),
                      reads=[btm, a.b_lng], writes=[a.b_v[i][q]])
        c.pump.pump(c.pump_rate)
        pend = []

        def flush_tr():
            while pend:
                i_, cidx_, gk = pend.pop(0)
                bank, bb = next_bank(c)
                pv = bank[:].bitcast(BF16)[:, 0:512].rearrange("p (k t) -> p k t", k=4)
                for k in range(4):
                    fw.op("pe", lambda e: e.transpose(out=pv[:, k, :], in_=a.gtc[gk][:, k * 128:(k + 1) * 128], identity=c.idn[:]),
                          reads=[a.b_gtc[gk], c.b_idn], writes=[bb])
                dst = a.v[i_][:, cidx_ * 512:(cidx_ + 1) * 512].rearrange("p (k t) -> p k t", k=4)
                if (i_ + cidx_) % 2 == 0:
                    fw.op("act", lambda e: e.activation(out=dst, in_=pv, func=AF.Copy), reads=[bb], writes=[a.b_v[i_][cidx_]])
                else:
                    fw.op("dve", lambda e: e.tensor_copy(out=dst, in_=pv), reads=[bb], writes=[a.b_v[i_][cidx_]])

        for cidx in range(8):
            Wu, bWu = load_W(c, a, wb_in, wbufs_in, 0, cidx)
            Wz, bWz = load_W(c, a, wb_in, wbufs_in, 0, 16 + cidx)
            pre = {}
            if cidx == 0:
                for i in range(2):
                    pu, bpu = next_bank(c)
                    mm16(pu, bpu, lambda k: hT[:, k, i * 128:(i + 1) * 128], Wu, bWu, [b_hT])
                    pz, bpz = next_bank(c)
                    mm16(pz, bpz, lambda k: hT[:, k, i * 128:(i + 1) * 128], Wz, bWz, [b_hT])
                    pre[i] = (pu, bpu, pz, bpz)
            for i in range(4):
                pm, bpm = next_bank(c)
                for gg in range(2):
                    g = 2 * cidx + gg
                    fw.op("pe", lambda e: e.matmul(pm[:, gg * 256:(gg + 1) * 256], lhsT=a.wsT[:, g, :],
                                                   rhs=a.v[i][:, g * 256:(g + 1) * 256], start=True, stop=True),
                          reads=[a.b_wsT, a.b_v[i][cidx]], writes=[bpm])
                if i in pre:
                    pu, bpu, pz, bpz = pre[i]
                else:
                    pu, bpu = next_bank(c)
                    mm16(pu, bpu, lambda k: hT[:, k, i * 128:(i + 1) * 128], Wu, bWu, [b_hT])
                    pz, bpz = next_bank(c)
                    mm16(pz, bpz, lambda k: hT[:, k, i * 128:(i + 1) * 128], Wz, bWz, [b_hT])
                flush_tr()
                r = a.rr; a.rr = 1 - r
                t1, bt1, t2, bt2, t3, bt3 = a.t1[r], a.b_t1[r], a.t2[r], a.b_t2[r], a.t3[r], a.b_t3[r]
                cs = slice(cidx * 512, (cidx + 1) * 512)
                fw.op("act", lambda e: e.activation(out=t1[:], in_=pu[:, :], func=AF.Gelu), reads=[bpu], writes=[bt1])
                fw.op("act", lambda e: e.activation(out=t2[:], in_=pz[:, :], func=AF.Silu), reads=[bpz], writes=[bt2])
                fw.op("dve", lambda e: e.tensor_tensor(out=t3[:], in0=pm[:, :], in1=Bm[:, cs], op=ALU.add), reads=[bpm, b_Bm], writes=[bt3])
                fw.op("dve", lambda e: e.tensor_tensor(out=t1[:], in0=t1[:], in1=t2[:], op=ALU.mult), reads=[bt1, bt2], writes=[bt1])
                gk = a.gtr; a.gtr = (gk + 1) % 3
                fw.op("dve", lambda e: e.tensor_tensor(out=a.gtc[gk][:], in0=t1[:], in1=t3[:], op=ALU.mult), reads=[bt1, bt3], writes=[a.b_gtc[gk]])
                pend.append((i, cidx, gk))
            if gi + 1 < ngroups:
                tn = (gi + 1) * 4 + cidx // 2
                if cidx % 2 == 0:
                    rms_pre(c, a, x_in[tn * 128:(tn + 1) * 128, :], c.bX[tn], a.g_bc, a.b_gbc)
                else:
                    rms_tr(c, a, cidx // 2, a.hTs[(gi + 1) % 2], a.b_hTs[(gi + 1) % 2])
        flush_tr()
        for nq in range(4):
            pys = [next_bank(c) for _ in range(4)]
            xslots = []
            for i in range(4):
                t = gi * 4 + i
                r = a.xrr; a.xrr = (r + 1) % 4
                xslots.append(r)
                fw.dma("act", a.xr[r][:], x_in[t * 128:(t + 1) * 128, nq * 512:(nq + 1) * 512], reads=[c.bX[t]], writes=[a.b_xr[r]], semkey="xr%d" % r)
            for kh in range(2):
                W, bW = load_W(c, a, wb_out, wbufs_out, kh, nq)
                for i in range(4):
                    py, bpy = pys[i]
                    mm16(py, bpy, lambda k: a.v[i][:, (16 * kh + k) * 128:(16 * kh + k + 1) * 128], W, bW,
                         lambda k: [a.b_v[i][(16 * kh + k) // 4]], start_first=(kh == 0), stop_last=(kh == 1))
            for i in range(4):
                t = gi * 4 + i
                py, bpy = pys[i]
                r = xslots[i]
                rows = slice(t * 128, (t + 1) * 128)
                cols = slice(nq * 512, (nq + 1) * 512)
                fw.op("dve", lambda e: e.tensor_tensor(out=a.xr[r][:], in0=py[:, :], in1=a.xr[r][:], op=ALU.add),
                      reads=[bpy, a.b_xr[r]], writes=[a.b_xr[r]])
                fw.dma("act", x_out[rows, cols], a.xr[r][:], reads=[a.b_xr[r]], writes=[c.bX[t]], semkey="xr%d" % r)


BW = 2048
DIL = (1, 4, 16)
SCALE = float(1.0 / np.sqrt(128.0))
BIG = 30000.0


def alibi_a(g, h):
    n = np.float32(g * 16 + h + 1)
    slope = np.power(np.float32(2.0), np.float32(-8.0) * n / np.float32(48))
    return float(np.float32(slope) * np.float32(DIL[g]))


def convert_B_in(c, w_ap, name):
    nc = c.nc
    wb = nc.dram_tensor(name, [16, 128, 16 * 10 * 128], BF16).ap()
    bufs = []
    end = 0
    for h in range(16):
        b = Buf("%s_%d" % (name, h))
        dst_h = wb[h].rearrange("p (k m n) -> p k m n", k=16, m=10)
        for m in range(10):
            src = w_ap[:, m * 2048 + h * 128: m * 2048 + (h + 1) * 128].rearrange("(k p) n -> p k n", p=128)
            end = c.pump.add(dst_h[:, :, m, :], src, b)
        bufs.append(b)
    return wb, bufs, end


def phase_B0(c, es, j, P, x_in, hTd, b_hTd):
    fw = c.fw
    sb = lambda name, shape, dt: es.enter_context(c.nc.sbuf_tensor(uniq(name), shape, dt))
    aa = []
    for i in range(2):
        a = Ctx()
        a.xt = sb("b0_xt%d" % i, [128, D], F32); a.b_xt = Buf("b0_xt")
        a.hb = sb("b0_hb%d" % i, [128, D], BF16); a.b_hb = Buf("b0_hb")
        a.st = sb("b0_st%d" % i, [128, 16], F32); a.b_st = Buf("b0_st")
        a.xkey = "B0_xt%d" % i
        aa.append(a)
    gbc = sb("b0_gbc", [128, D], F32); b_gbc = Buf("b0_gbc")
    hT = [sb("b0_hT%d" % i, [128, 16, 512], BF16) for i in range(2)]
    b_hT = [Buf("b0_hT%d" % i) for i in range(2)]
    fw.dma("sp", gbc[:], P["b_norm_g"][j].partition_broadcast(128), writes=[b_gbc], semkey="gbc")
    NT = c.NT
    rms_pre(c, aa[0], x_in[0:128, :], c.bX[0], gbc, b_gbc, semkey=aa[0].xkey)
    for t in range(NT):
        gi, i = divmod(t, 4)
        s_ = gi % 2
        if t + 1 < NT:
            rms_pre(c, aa[(t + 1) % 2], x_in[(t + 1) * 128:(t + 2) * 128, :], c.bX[t + 1], gbc, b_gbc, semkey=aa[(t + 1) % 2].xkey)
        rms_tr(c, aa[t % 2], i, hT[s_], b_hT[s_])
        if i == 3:
            fw.dma("sp", hTd[:, :, gi * 512:(gi + 1) * 512], hT[s_][:], reads=[b_hT[s_]], writes=[b_hTd[gi]], semkey="B0_hT%d" % s_)


def phase_B1(c, es, j, wb, wbufs, hTd, b_hTd, ozTd, b_ozTd):
    fw = c.fw
    nc = c.nc
    T = c.T
    NSB = T // 2048
    sb_ = lambda name, shape, dt: es.enter_context(nc.sbuf_tensor(uniq(name), shape, dt))
    Wh = sb_("b1_Wh", [128, 16, 10, 128], BF16); b_Wh = Buf("b1_Wh")
    hTg = [sb_("b1_hTg%d" % i, [128, 16, 512], BF16) for i in range(2)]; b_hTg = [Buf("b1_hTg%d" % i) for i in range(2)]
    KT = [sb_("b1_KT%d" % g, [128, T], BF16) for g in range(3)]
    V = [sb_("b1_V%d" % g, [128, T // 128, 128], BF16) for g in range(3)]
    QT = [sb_("b1_QT%d" % g, [128, 2048], BF16) for g in range(3)]
    sz = sb_("b1_sz", [128, 2048], F32)
    VT2 = sb_("b1_VT2", [128, 2048], BF16); b_VT2 = Buf("b1_VT2")
    VTt = [sb_("b1_VTt%d" % i, [128, 512], BF16) for i in range(2)]; b_VTt = [Buf("b1_VTt%d" % i) for i in range(2)]
    accs = [sb_("b1_acc%d" % i, [128, 2, 2048], F32) for i in range(2)]; b_accs = [Buf("b1_acc%d" % i) for i in range(2)]
    Et = [sb_("b1_E%d" % i, [128, 256], BF16) for i in range(5)]; b_E = [Buf("b1_E%d" % i) for i in range(5)]
    tmp = [sb_("b1_tmp%d" % i, [128, 256], F32) for i in range(5)]; b_tmp = [Buf("b1_tmp%d" % i) for i in range(5)]
    distf = sb_("b1_dist", [128, 256], F32); b_dist = Buf("b1_dist")
    ones = sb_("b1_ones", [128, 128], BF16); b_ones = Buf("b1_ones")
    ob = [sb_("b1_ob%d" % i, [128, 2048], BF16) for i in range(2)]; b_ob = [Buf("b1_ob%d" % i) for i in range(2)]
    fw.op("pool", lambda e: e.memset(ones[:], 1.0), writes=[b_ones])
    fw.op("pool", lambda e: e.iota(distf[:, 0:128], pattern=[[1, 128]], base=128, channel_multiplier=-1, allow_small_or_imprecise_dtypes=True), writes=[b_dist])
    fw.op("pool", lambda e: e.iota(distf[:, 128:256], pattern=[[1, 128]], base=0, channel_multiplier=-1, allow_small_or_imprecise_dtypes=True), writes=[b_dist])
    fw.op("pool", lambda e: e.affine_select(out=distf[:, 0:128], in_=distf[:, 0:128], pattern=[[-1, 128]], compare_op=ALU.is_ge,
                                            fill=BIG, base=0, channel_multiplier=1), reads=[b_dist], writes=[b_dist])
    fw.op("pool", lambda e: e.affine_select(out=distf[:, 128:256], in_=distf[:, 128:256], pattern=[[1, 128]], compare_op=ALU.is_ge,
                                            fill=BIG, base=0, channel_multiplier=-1), reads=[b_dist], writes=[b_dist])
    rr = {"hTg": 0, "vt": 0, "e": 0, "ob": 0, "ev": 0, "acc": 0}
    NG = T // 512
    b_KT = [[Buf("b1_KT%d_%d" % (g, i)) for i in range(NG)] for g in range(3)]
    b_V = [[Buf("b1_V%d_%d" % (g, i)) for i in range(NG)] for g in range(3)]
    b_QT = [[Buf("b1_QT%d_%d" % (g, i)) for i in range(4)] for g in range(3)]
    b_szg = [Buf("b1_sz_%d" % i) for i in range(4)]

    def evac(dst, src_bank, bb, wbuf, func=None):
        if func is not None:
            fw.op("act", lambda e: e.activation(out=dst, in_=src_bank, func=func), reads=[bb], writes=[wbuf])
            return
        rr["ev"] ^= 1
        if rr["ev"]:
            fw.op("act", lambda e: e.activation(out=dst, in_=src_bank, func=AF.Copy), reads=[bb], writes=[wbuf])
        else:
            fw.op("dve", lambda e: e.tensor_copy(out=dst, in_=src_bank), reads=[bb], writes=[wbuf])

    def v_transposes(g, src, b_src, cols_of, blks, wbuf):
        n = len(blks)
        for q0 in range(0, n, 8):
            bank, bb = next_bank(c)
            pv = bank[:].bitcast(BF16).rearrange("p (k t) -> p k t", k=8)
            m = min(8, n - q0)
            for k in range(m):
                fw.op("pe", lambda e: e.transpose(out=pv[:, k, :], in_=cols_of(q0 + k), identity=c.idn[:]),
                      reads=[b_src, c.b_idn], writes=[bb])
            evac(V[g][:, blks[q0]:blks[q0] + m, :], pv[:, 0:m, :], bb, wbuf)

    LAG = 3
    NE = len(Et)
    deferred = []

    def flush_deferred():
        while deferred:
            h_, sbi_, os__ = deferred.pop(0)
            fw.dma("sp", ozTd[h_, :, sbi_ * 2048:(sbi_ + 1) * 2048], ob[os__][:], reads=[b_ob[os__]], writes=[b_ozTd[h_]],
                   semkey="B1_ob%d" % os__)

    for h in range(16):
        fw.dma("sp", Wh[:].rearrange("p k m n -> p (k m n)"), wb[h], reads=[wbufs[h]], writes=[b_Wh], semkey="B1_Wh")
        for sbi in range(NSB):
            ai = rr["acc"]; rr["acc"] ^= 1
            acc, b_acc = accs[ai], b_accs[ai]
            items = []
            st = {"i1": 0, "i2": 0}
            st1 = {}

            def item_info(g, jl, r):
                d = DIL[g]
                span = 128 * d
                nj = 2048 // span
                jg = sbi * nj + jl
                if g == 0:
                    gl_ = jl // 4
                    qb = [b_QT[0][gl_]]
                    kc = [b_KT[0][jg // 4]]; kp = [b_KT[0][(jg - 1) // 4]] if jg > 0 else []
                    vc = [b_V[0][jg // 4]]; vp = [b_V[0][(jg - 1) // 4]] if jg > 0 else []
                elif g == 1:
                    qb = [b_QT[1][jl]]
                    kc = [b_KT[1][jg]]; kp = [b_KT[1][jg - 1]] if jg > 0 else []
                    vc = [b_V[1][jg]]; vp = [b_V[1][jg - 1]] if jg > 0 else []
                else:
                    qb = list(b_QT[2])
                    kc = [b_KT[2][sbi * 4 + i] for i in range(4)]
                    kp = [b_KT[2][(sbi - 1) * 4 + i] for i in range(4)] if jg > 0 else []
                    vc = [b_V[2][sbi]]; vp = [b_V[2][sbi - 1]] if jg > 0 else []
                return d, span, jg, qb, kc, kp, vc, vp

            def stage1(idx):
                g, jl, r = items[idx]
                d, span, jg, qb, kc, kp, vc, vp = item_info(g, jl, r)
                coef = -alibi_a(g, h) / SCALE
                qcols = QT[g][:, jl * span + r: (jl + 1) * span: d]
                kcur = KT[g][:, jg * span + r: (jg + 1) * span: d]
                has_prev = jg > 0
                ps, bps = next_bank(c)
                if has_prev:
                    kprev = KT[g][:, (jg - 1) * span + r: jg * span: d]
                    fw.op("pe", lambda e: e.matmul(ps[:, 0:128], lhsT=kprev, rhs=qcols, start=True, stop=True),
                          reads=kp + qb, writes=[bps])
                fw.op("pe", lambda e: e.matmul(ps[:, 128:256], lhsT=kcur, rhs=qcols, start=True, stop=True),
                      reads=kc + qb, writes=[bps])
                es_ = rr["e"]; rr["e"] = (es_ + 1) % NE
                cs = slice(0, 256) if has_prev else slice(128, 256)
                fw.op("dve", lambda e: e.scalar_tensor_tensor(out=tmp[es_][:, cs], in0=distf[:, cs], scalar=coef, in1=ps[:, cs],
                                                              op0=ALU.mult, op1=ALU.add),
                      reads=[b_dist, bps], writes=[b_tmp[es_]])
                fw.op("act", lambda e: e.activation(out=Et[es_][:, cs], in_=tmp[es_][:, cs], func=AF.Exp, scale=SCALE),
                      reads=[b_tmp[es_]], writes=[b_E[es_]])
                st1[idx] = es_

            def stage2(idx):
                g, jl, r = items[idx]
                d, span, jg, qb, kc, kp, vc, vp = item_info(g, jl, r)
                has_prev = jg > 0
                es_ = st1.pop(idx)
                blk_cur = jg * d + r
                po, bpo = next_bank(c)
                first = True
                for is_ones in (False, True):
                    oc = slice(128, 256) if is_ones else slice(0, 128)
                    if has_prev:
                        l0 = ones[:] if is_ones else V[g][:, blk_cur - d, :]
                        fw.op("pe", lambda e: e.matmul(po[:, oc], lhsT=l0, rhs=Et[es_][:, 0:128], start=first, stop=False,
                                                       skip_group_check=True),
                              reads=([b_ones] if is_ones else vp) + [b_E[es_]], writes=[bpo])
                        first = False
                    l1 = ones[:] if is_ones else V[g][:, blk_cur, :]
                    fw.op("pe", lambda e: e.matmul(po[:, oc], lhsT=l1, rhs=Et[es_][:, 128:256], start=first, stop=True,
                                                   skip_group_check=True),
                          reads=([b_ones] if is_ones else vc) + [b_E[es_]], writes=[bpo])
                    first = False
                pov = po[:, 0:256].rearrange("p (a q) -> p a q", a=2)
                dst = acc[:, :, jl * span + r: (jl + 1) * span: d]
                if g == 0:
                    fw.op("act", lambda e: e.activation(out=dst, in_=pov, func=AF.Copy), reads=[bpo], writes=[b_acc])
                else:
                    fw.op("dve", lambda e: e.tensor_tensor(out=dst, in0=pov, in1=dst, op=ALU.add), reads=[bpo, b_acc], writes=[b_acc])

            def step(drain=False):
                did = False
                if st["i1"] < len(items):
                    stage1(st["i1"]); st["i1"] += 1
                    did = True
                if st["i2"] < st["i1"] and (st["i1"] - st["i2"] > LAG or (drain and st["i1"] == len(items))):
                    stage2(st["i2"]); st["i2"] += 1
                    did = True
                return did

            for gl in range(4):
                gidx = sbi * 4 + gl
                s = rr["hTg"]; rr["hTg"] ^= 1
                fw.dma("sp", hTg[s][:], hTd[:, :, gidx * 512:(gidx + 1) * 512], reads=[b_hTd[gidx]], writes=[b_hTg[s]],
                       semkey="B1_hTg%d" % s)
                if gl == 2:
                    flush_deferred()
                lc = slice(gl * 512, (gl + 1) * 512)
                gc = slice(gidx * 512, (gidx + 1) * 512)
                for m in range(10):
                    bank, bb = next_bank(c)
                    for k in range(16):
                        fw.op("pe", lambda e: e.matmul(bank[:, :], lhsT=Wh[:, k, m, :], rhs=hTg[s][:, k, :], start=(k == 0), stop=(k == 15)),
                              reads=[b_Wh, b_hTg[s]], writes=[bb])
                    g, kind = divmod(m, 3)
                    if m == 9:
                        evac(sz[:, lc], bank[:, :], bb, b_szg[gl], func=AF.Silu)
                    elif kind == 0:
                        evac(QT[g][:, lc], bank[:, :], bb, b_QT[g][gl])
                    elif kind == 1:
                        evac(KT[g][:, gc], bank[:, :], bb, b_KT[g][gidx])
                    else:
                        if g == 2:
                            evac(VT2[:, lc], bank[:, :], bb, b_VT2)
                        else:
                            vs = rr["vt"]; rr["vt"] ^= 1
                            evac(VTt[vs][:], bank[:, :], bb, b_VTt[vs])
                            if g == 0:
                                v_transposes(0, VTt[vs], b_VTt[vs], lambda i: VTt[vs][:, i * 128:(i + 1) * 128],
                                             [gidx * 4 + i for i in range(4)], b_V[0][gidx])
                            else:
                                v_transposes(1, VTt[vs], b_VTt[vs], lambda r: VTt[vs][:, r:512:4],
                                             [gidx * 4 + r for r in range(4)], b_V[1][gidx])
                    step()
                for i in range(4):
                    items.append((0, gl * 4 + i, 0))
                for r in range(4):
                    items.append((1, gl, r))
            v_transposes(2, VT2, b_VT2, lambda r: VT2[:, r:2048:16], [sbi * 16 + r for r in range(16)], b_V[2][sbi])
            for r in range(16):
                items.append((2, 0, r))
            while step(drain=True):
                pass
            assert st["i2"] == len(items) and not st1
            c.pump.pump(c.pump_rate)
            fw.op("dve", lambda e: e.reciprocal(out=acc[:, 1, :], in_=acc[:, 1, :]), reads=[b_acc], writes=[b_acc])
            fw.op("dve", lambda e: e.tensor_tensor(out=acc[:, 0, :], in0=acc[:, 0, :], in1=acc[:, 1, :], op=ALU.mult), reads=[b_acc], writes=[b_acc])
            os_ = rr["ob"]; rr["ob"] ^= 1
            fw.op("dve", lambda e: e.tensor_tensor(out=ob[os_][:], in0=acc[:, 0, :], in1=sz[:], op=ALU.mult), reads=[b_acc] + b_szg, writes=[b_ob[os_]])
            deferred.append((h, sbi, os_))
    flush_deferred()


def phase_B3(c, es, wbo, wbufs_o, ozTd, b_ozTd, x_in, x_out):
    fw = c.fw
    nc = c.nc
    sb_ = lambda name, shape, dt: es.enter_context(nc.sbuf_tensor(uniq(name), shape, dt))
    ozg = [sb_("b3_ozg%d" % i, [128, 16, 512], BF16) for i in range(2)]; b_ozg = [Buf("b3_ozg%d" % i) for i in range(2)]
    W = [sb_("b3_W%d" % i, [128, 16, 512], BF16) for i in range(4)]; b_W = [Buf("b3_W%d" % i) for i in range(4)]
    xr = [sb_("b3_xr%d" % i, [128, 2048], F32) for i in range(3)]; b_xr = [Buf("b3_xr%d" % i) for i in range(3)]
    for nq in range(4):
        fw.dma("sp", W[nq][:].rearrange("p k n -> p (k n)"), wbo[0, nq], reads=[wbufs_o[(0, nq)]], writes=[b_W[nq]], semkey="A_W%d" % (nq % 3) if nq < 3 else "B3_W3")
    rr = 0
    for gi in range(c.NT // 4):
        s = gi % 2
        fw.dma("sp", ozg[s][:], ozTd[:, :, gi * 512:(gi + 1) * 512].rearrange("h p t -> p h t"), reads=b_ozTd, writes=[b_ozg[s]], semkey="B3_ozg%d" % s)
        for i in range(4):
            t = gi * 4 + i
            r = rr; rr = (rr + 1) % 3
            rows = slice(t * 128, (t + 1) * 128)
            fw.dma("act", xr[r][:], x_in[rows, :], reads=[c.bX[t]], writes=[b_xr[r]], semkey="xr%d" % r)
            for nq in range(4):
                bank, bb = next_bank(c)
                for k in range(16):
                    fw.op("pe", lambda e: e.matmul(bank[:, :], lhsT=ozg[s][:, k, i * 128:(i + 1) * 128], rhs=W[nq][:, k, :],
                                                   start=(k == 0), stop=(k == 15)), reads=[b_ozg[s], b_W[nq]], writes=[bb])
                cols = slice(nq * 512, (nq + 1) * 512)
                fw.op("dve", lambda e: e.tensor_tensor(out=xr[r][:, cols], in0=bank[:, :], in1=xr[r][:, cols], op=ALU.add), reads=[bb, b_xr[r]], writes=[b_xr[r]])
            fw.dma("act", x_out[rows, :], xr[r][:], reads=[b_xr[r]], writes=[c.bX[t]], semkey="xr%d" % r)


def phase_final(c, es, P, x_in, x_out):
    fw = c.fw
    sb_ = lambda name, shape, dt: es.enter_context(c.nc.sbuf_tensor(uniq(name), shape, dt))
    xt = [sb_("f_xt%d" % i, [128, D], F32) for i in range(2)]; b_xt = [Buf("f_xt%d" % i) for i in range(2)]
    jk = sb_("f_jk", [128, D], BF16); b_jk = Buf("f_jk")
    st = sb_("f_st", [128, 4], F32); b_st = Buf("f_st")
    gbc = sb_("f_gbc", [128, D], F32); b_gbc = Buf("f_gbc")
    fw.dma("sp", gbc[:], P["final_norm_g"].partition_broadcast(128), writes=[b_gbc], semkey="gbc")
    for t in range(c.NT):
        s = t % 2
        rows = slice(t * 128, (t + 1) * 128)
        fw.dma("sp", xt[s][:], x_in[rows, :], reads=[c.bX[t]], writes=[b_xt[s]], semkey="xr%d" % s)
        fw.op("act", lambda e: e.activation(out=jk[:], in_=xt[s][:], func=AF.Square, scale=float(D ** -0.5), accum_out=st[:, 0:1]),
              reads=[b_xt[s]], writes=[b_jk, b_st])
        fw.op("act", lambda e: e.activation(out=st[:, 1:2], in_=st[:, 0:1], func=AF.Sqrt, bias=EPS, scale=1.0), reads=[b_st], writes=[b_st])
        fw.op("dve", lambda e: e.reciprocal(out=st[:, 2:3], in_=st[:, 1:2]), reads=[b_st], writes=[b_st])
        fw.op("dve", lambda e: e.scalar_tensor_tensor(out=xt[s][:], in0=xt[s][:], scalar=st[:, 2:3], in1=gbc[:], op0=ALU.mult, op1=ALU.mult),
              reads=[b_xt[s], b_st, b_gbc], writes=[b_xt[s]])
        fw.dma("sp", x_out[rows, :], xt[s][:], reads=[b_xt[s]], writes=[c.bX[t]], semkey="xr%d" % s)


T_CORE = 4096
N_CORES = 4
from concourse.bass_utils import run_bass_kernel_spmd


def build_program(n_pairs=2, final=True):
    T = T_CORE
    nc = bass.Bass("TRN2", target_bir_lowering=False)
    P = {}

    def inp(name, shape):
        P[name] = nc.dram_tensor(name, shape, F32, kind="ExternalInput").ap()
    inp("x", [T, 2048])
    inp("a_norm_g", [2, 2048]); inp("a_w_in", [2, 2048, 12288]); inp("a_ln_g", [2, 4096]); inp("a_ln_b", [2, 4096])
    inp("a_w_s", [2, 16, 128, 128]); inp("a_b_s", [2, 16, 128]); inp("a_w_out", [2, 4096, 2048])
    inp("b_norm_g", [2, 2048]); inp("b_w_in", [2, 2048, 20480]); inp("b_w_out", [2, 2048, 2048]); inp("final_norm_g", [2048])
    out = nc.dram_tensor("out", [T, 2048], F32, kind="ExternalOutput").ap()
    hTd = nc.dram_tensor("hTd", [128, 16, T], BF16).ap()
    ozTd = nc.dram_tensor("ozTd", [16, 128, T], BF16).ap()
    es = ExitStack()
    with es:
        fw = FW(nc, es)
        c = setup_common(nc, es, fw, T)
        conv = []
        a_order_in = [(0, 8 + i) for i in range(8)]
        for i in range(8):
            a_order_in += [(0, i), (0, 16 + i)]
        a_order_out = [(kh, nq) for nq in range(4) for kh in range(2)]
        for j in range(2):
            wa_in = convert_weights(c, P["a_w_in"][j], 2048, 12288, "wb_a_in%d" % j, order=a_order_in)
            wa_out = convert_weights(c, P["a_w_out"][j], 4096, 2048, "wb_a_out%d" % j, order=a_order_out)
            wb_in = convert_B_in(c, P["b_w_in"][j], "wb_b_in%d" % j)
            wb_out = convert_weights(c, P["b_w_out"][j], 2048, 2048, "wb_b_out%d" % j)
            conv.append((wa_in, wa_out, wb_in, wb_out))
        x_cur = P["x"]
        for j in range(n_pairs):
            wa_in, wa_out, wb_in, wb_out = conv[j]
            c.pump.ensure(wa_out[2])
            c.pump_rate = 22 if j == 0 else 0
            with ExitStack() as es2:
                a = setup_A(c, es2)
                phase_A(c, a, j, P, x_cur, out, wa_in[0], wa_in[1], wa_out[0], wa_out[1])
                fw.barrier()
            x_cur = out
            c.pump.ensure(wb_out[2])
            c.pump_rate = 7 if j == 0 else 0
            b_hTd = [Buf("hTd%d" % i) for i in range(T // 512)]
            b_ozTd = [Buf("ozTd%d" % i) for i in range(16)]
            with ExitStack() as es2:
                phase_B0(c, es2, j, P, x_cur, hTd, b_hTd)
                fw.barrier()
            with ExitStack() as es2:
                phase_B1(c, es2, j, wb_in[0], wb_in[1], hTd, b_hTd, ozTd, b_ozTd)
                fw.barrier()
            with ExitStack() as es2:
                phase_B3(c, es2, wb_out[0], wb_out[1], ozTd, b_ozTd, x_cur, out)
                fw.barrier()
        if n_pairs == 2:
            c.pump.ensure(len(c.pump.q))
        with ExitStack() as es2:
            phase_final(c, es2, P, out, out)
            fw.barrier()
        fw.finish(c.bX, e="sp")
        fw.finish(c.bX, e="act")
    return nc


def kernel(**inputs):
    x = np.ascontiguousarray(np.asarray(inputs["x"], dtype=np.float32))
    nc = build_program()
    shared = {k: np.ascontiguousarray(np.asarray(v, dtype=np.float32)) for k, v in inputs.items() if k != "x"}
    in_maps = []
    for b in range(N_CORES):
        m = dict(shared)
        m["x"] = x[b]
        in_maps.append(m)
    res = run_bass_kernel_spmd(nc, in_maps, core_ids=list(range(N_CORES)))
    return np.stack([np.asarray(res.results[b]["out"], dtype=np.float32) for b in range(N_CORES)], axis=0)
```

```python
import numpy as np
import concourse.bass as bass
import concourse.mybir as mybir
from contextlib import ExitStack
F32 = mybir.dt.float32; BF16 = mybir.dt.bfloat16
AF = mybir.ActivationFunctionType
ALU = mybir.AluOpType
AX = mybir.AxisListType


_UID = [0]


def uniq(name):
    _UID[0] += 1
    return "%s_u%d" % (name, _UID[0])


class Buf:
    __slots__ = ("name", "w", "r")

    def __init__(self, name):
        self.name = name
        self.w = None
        self.r = {}


class FW:
    def __init__(self, nc, es, n_dma_sems=50):
        self.nc = nc
        self.es = es
        self.eng = {"pe": nc.tensor, "act": nc.scalar, "dve": nc.vector, "pool": nc.gpsimd, "sp": nc.sync}
        self.sem = {}
        self.cnt = {}
        for k in ("pe", "act", "dve", "pool"):
            self.sem[k] = es.enter_context(nc.semaphore("sem_" + k))
            self.cnt[k] = 0
        self.n_dma = 0
        self.waited = {k: {} for k in self.eng}
        self.dma_free = []
        self.n_dma_sems = n_dma_sems
        self.dma_rr = 0
        self.q_rr = {}
        self.keymap = {}
        self.ninst = 0

    def _need(self, e, deps):
        need = {}
        for d in deps:
            if d is None:
                continue
            k, v = d
            if need.get(k, 0) < v:
                need[k] = v
        w = self.waited[e]
        eng = self.eng[e]
        for k, v in need.items():
            if k == e and e == "pe":
                continue
            if w.get(k, 0) >= v:
                continue
            eng.wait_ge(self.sem[k], v)
            w[k] = v
            self.ninst += 1

    def _deps(self, reads, writes):
        deps = []
        for b in reads:
            deps.append(b.w)
        for b in writes:
            deps.append(b.w)
            for k, v in b.r.items():
                deps.append((k, v))
        return deps

    def _mark(self, ev, reads, writes):
        k, v = ev
        for b in reads:
            b.r[k] = v
        for b in writes:
            b.w = ev
            b.r = {}

    def op(self, e, fn, reads=(), writes=()):
        self._need(e, self._deps(reads, writes))
        ins = fn(self.eng[e])
        self.cnt[e] += 1
        ins.then_inc(self.sem[e], 1)
        self._mark((e, self.cnt[e]), reads, writes)
        self.ninst += 1
        return ins

    def dma(self, q, out, in_, reads=(), writes=(), semkey=None):
        self._need(q, self._deps(reads, writes))
        if semkey is None:
            raise ValueError("dma needs a semkey (one in-flight DMA per key)")
        if semkey not in self.keymap:
            assert len(self.keymap) < self.n_dma_sems, "out of DMA semaphores"
            k = "dma%d" % len(self.keymap)
            self.keymap[semkey] = k
            self.sem[k] = self.es.enter_context(self.nc.semaphore("sem_" + k))
            self.cnt[k] = 0
        semkey = self.keymap[semkey]
        ins = self.eng[q].dma_start(out=out, in_=in_)
        self.cnt[semkey] += 16
        ins.then_inc(self.sem[semkey], 16)
        self._mark((semkey, self.cnt[semkey]), reads, writes)
        self.ninst += 1
        return ins

    def finish(self, bufs, e="sp"):
        self._need(e, [b.w for b in bufs])


def fw_barrier(fw):
    evs = [(k, v) for k, v in fw.cnt.items() if v > 0]
    for e in ("pe", "act", "dve", "pool", "sp"):
        w = fw.waited[e]
        for k, v in evs:
            if w.get(k, 0) >= v:
                continue
            fw.eng[e].wait_ge(fw.sem[k], v)
            w[k] = v
FW.barrier = fw_barrier


D = 2048
AW = 4096
EPS = 1e-6
NCHUNK_IN_A = 24


class Ctx:
    pass


def setup_common(nc, es, fw, T):
    c = Ctx()
    c.nc, c.es, c.fw, c.T = nc, es, fw, T
    c.NT = T // 128
    sb = lambda name, shape, dt: es.enter_context(nc.sbuf_tensor(uniq(name), shape, dt))
    c.sb = sb
    c.banks = [es.enter_context(nc.psum_tensor("bank%d" % i, [128, 512], F32)) for i in range(8)]
    c.bbuf = [Buf("bank%d" % i) for i in range(8)]
    c.bank_rr = 0
    c.idf = sb("idf", [128, 128], F32); c.b_idf = Buf("idf")
    c.idn = sb("idn", [128, 128], BF16); c.b_idn = Buf("idn")
    fw.op("pool", lambda e: e.memset(c.idf[:], 1.0), writes=[c.b_idf])
    fw.op("pool", lambda e: e.affine_select(out=c.idf[:], in_=c.idf[:], pattern=[[-1, 128]], compare_op=ALU.is_equal,
                                            fill=0.0, base=0, channel_multiplier=1), reads=[c.b_idf], writes=[c.b_idf])
    fw.op("dve", lambda e: e.tensor_copy(out=c.idn[:], in_=c.idf[:]), reads=[c.b_idf], writes=[c.b_idn])
    c.bX = [Buf("X%d" % i) for i in range(c.NT)]
    c.pump = ConvPump(c)
    c.pump_rate = 0
    return c


def next_bank(c):
    i = c.bank_rr
    c.bank_rr = (i + 1) % 8
    return c.banks[i], c.bbuf[i]


class ConvPump:
    def __init__(self, c):
        self.c = c
        self.q = []
        self.pos = 0
        self.hist = []

    def add(self, dst, src, buf):
        self.q.append((dst, src, buf))
        return len(self.q)

    def pump(self, n, paced=True):
        fw = self.c.fw
        if paced and n > 0 and self.pos < len(self.q) and fw.cnt["pe"] > 0:
            fw._need("pool", [("pe", fw.cnt["pe"])])
        while n > 0 and self.pos < len(self.q):
            dst, src, buf = self.q[self.pos]
            self.pos += 1
            n -= 1
            if len(self.hist) >= 2:
                fw._need("pool", [self.hist[-2]])
            fw.dma("pool", dst, src, writes=[buf], semkey="conv%d" % (self.pos % 3))
            self.hist.append(buf.w)

    def ensure(self, upto):
        if self.pos < upto:
            self.pump(upto - self.pos, paced=False)


def convert_weights(c, w_ap, rows, cols, name, order=None):
    nc = c.nc
    nkh = rows // 2048
    ncc = cols // 512
    wb = nc.dram_tensor(name, [nkh, ncc, 128, 16 * 512], BF16).ap()
    bufs = {}
    if order is None:
        order = [(kh, cc) for kh in range(nkh) for cc in range(ncc)]
    end = 0
    for (kh, cc) in order:
        b = Buf("%s_%d_%d" % (name, kh, cc))
        src = w_ap[kh * 2048:(kh + 1) * 2048, cc * 512:(cc + 1) * 512].rearrange("(k p) n -> p k n", p=128)
        dst = wb[kh, cc].rearrange("p (k n) -> p k n", k=16)
        end = c.pump.add(dst, src, b)
        bufs[(kh, cc)] = b
    return wb, bufs, end


def setup_A(c, es):
    sb = lambda name, shape, dt: es.enter_context(c.nc.sbuf_tensor(uniq(name), shape, dt))
    a = Ctx()
    a.G = 4
    a.xt = sb("a_xt", [128, D], F32); a.b_xt = Buf("a_xt")
    a.hb = sb("a_hb", [128, D], BF16); a.b_hb = Buf("a_hb")
    a.st = sb("a_st", [128, 16], F32); a.b_st = Buf("a_st")
    a.hTs = [sb("a_hT%d" % i, [128, 16, 512], BF16) for i in range(2)]; a.b_hTs = [Buf("a_hT%d" % i) for i in range(2)]
    a.g_bc = sb("a_gbc", [128, D], F32); a.b_gbc = Buf("a_gbc")
    a.lng = sb("a_lng", [128, AW], F32); a.b_lng = Buf("a_lng")
    a.lnb = sb("a_lnb", [128, AW], F32); a.b_lnb = Buf("a_lnb")
    a.wsf = a.xt[:].rearrange("p (g s) -> p g s", g=16); a.b_wsf = a.b_xt
    a.wsb = a.hb[:].rearrange("p (g s) -> p g s", g=16); a.b_wsb = a.b_hb
    a.wsT = sb("a_wsT", [128, 16, 128], BF16); a.b_wsT = Buf("a_wsT")
    a.bs = sb("a_bs", [128, 16], F32); a.b_bs = Buf("a_bs")
    a.W = [sb("a_W%d" % i, [128, 16, 512], BF16) for i in range(3)]
    a.b_W = [Buf("a_W%d" % i) for i in range(3)]
    a.w_rr = 0
    a.v = [sb("a_v%d" % i, [128, AW], BF16) for i in range(4)]; a.b_v = [[Buf("a_v%d_%d" % (i, q)) for q in range(8)] for i in range(4)]
    a.gtc = [sb("a_gtc%d" % i, [128, 512], BF16) for i in range(3)]; a.b_gtc = [Buf("a_gtc%d" % i) for i in range(3)]
    a.gtr = 0
    a.t3 = [sb("a_t3_%d" % i, [128, 512], F32) for i in range(2)]; a.b_t3 = [Buf("a_t3_%d" % i) for i in range(2)]
    a.w1 = sb("a_w1", [128, 16], F32); a.b_w1 = Buf("a_w1")
    a.stats = [sb("a_stats%d" % i, [128, 8, 6], F32) for i in range(4)]; a.b_stats = [Buf("a_stats%d" % i) for i in range(4)]
    a.tmp32s = [sb("a_tmp32_%d" % i, [128, 512], F32) for i in range(3)]; a.b_tmp32s = [Buf("a_tmp32_%d" % i) for i in range(3)]
    a.tmr = 0
    a.mvs = [sb("a_mv%d" % i, [128, 8], F32) for i in range(4)]; a.b_mvs = [Buf("a_mv%d" % i) for i in range(4)]
    a.t1 = [sb("a_t1_%d" % i, [128, 512], F32) for i in range(2)]; a.b_t1 = [Buf("a_t1_%d" % i) for i in range(2)]
    a.t2 = [sb("a_t2_%d" % i, [128, 512], F32) for i in range(2)]; a.b_t2 = [Buf("a_t2_%d" % i) for i in range(2)]
    a.xr = [sb("a_xr%d" % i, [128, 512], F32) for i in range(4)]; a.b_xr = [Buf("a_xr%d" % i) for i in range(4)]
    a.xrr = 0
    a.rr = 0
    return a


def load_W(c, a, wb, wbufs, kh, cc):
    fw = c.fw
    i = a.w_rr
    a.w_rr = (i + 1) % 3
    fw.dma("sp", a.W[i][:].rearrange("p k n -> p (k n)"), wb[kh, cc], reads=[wbufs[(kh, cc)]], writes=[a.b_W[i]], semkey="A_W%d" % i)
    return a.W[i], a.b_W[i]


def rms_pre(c, a, x_src, bX, gbc, b_gbc, semkey="xt"):
    fw = c.fw
    fw.dma("sp", a.xt[:], x_src, reads=[bX], writes=[a.b_xt], semkey=semkey)
    fw.op("act", lambda e: e.activation(out=a.hb[:], in_=a.xt[:], func=AF.Square, scale=float(D ** -0.5),
                                        accum_out=a.st[:, 0:1]), reads=[a.b_xt], writes=[a.b_hb, a.b_st])
    fw.op("act", lambda e: e.activation(out=a.st[:, 1:2], in_=a.st[:, 0:1], func=AF.Sqrt, bias=EPS, scale=1.0),
          reads=[a.b_st], writes=[a.b_st])
    fw.op("dve", lambda e: e.reciprocal(out=a.st[:, 2:3], in_=a.st[:, 1:2]), reads=[a.b_st], writes=[a.b_st])
    fw.op("dve", lambda e: e.scalar_tensor_tensor(out=a.hb[:], in0=a.xt[:], scalar=a.st[:, 2:3], in1=gbc[:],
                                                  op0=ALU.mult, op1=ALU.mult),
          reads=[a.b_xt, a.b_st, b_gbc], writes=[a.b_hb])


def rms_tr(c, a, tile_in_group, hT, b_hT):
    fw = c.fw
    t0 = tile_in_group * 128
    for half in range(2):
        bank, bb = next_bank(c)
        pv = bank[:].bitcast(BF16).rearrange("p (k t) -> p k t", k=8)
        for k in range(8):
            kk = half * 8 + k
            fw.op("pe", lambda e: e.transpose(out=pv[:, k, :], in_=a.hb[:, kk * 128:(kk + 1) * 128], identity=c.idn[:]),
                  reads=[a.b_hb, c.b_idn], writes=[bb])
        if half == 0:
            fw.op("act", lambda e: e.activation(out=hT[:, half * 8:(half + 1) * 8, t0:t0 + 128], in_=pv, func=AF.Copy),
                  reads=[bb], writes=[b_hT])
        else:
            fw.op("dve", lambda e: e.tensor_copy(out=hT[:, half * 8:(half + 1) * 8, t0:t0 + 128], in_=pv),
                  reads=[bb], writes=[b_hT])


def rms_to_hT(c, a, x_src, bX, tile_in_group, hT, b_hT, gbc, b_gbc, hw=512):
    rms_pre(c, a, x_src, bX, gbc, b_gbc)
    rms_tr(c, a, tile_in_group, hT, b_hT)


def layer_A_consts(c, a, j, P):
    fw = c.fw
    fw.dma("sp", a.g_bc[:], P["a_norm_g"][j].partition_broadcast(128), writes=[a.b_gbc], semkey="gbc")
    fw.dma("sp", a.lng[:], P["a_ln_g"][j].partition_broadcast(128), writes=[a.b_lng], semkey="lng")
    fw.dma("sp", a.lnb[:], P["a_ln_b"][j].partition_broadcast(128), writes=[a.b_lnb], semkey="lnb")
    fw.dma("sp", a.wsf, P["a_w_s"][j].rearrange("g t s -> t g s"), writes=[a.b_wsf], semkey="xt")
    with c.nc.allow_non_contiguous_dma(reason="tiny bias transpose"):
        fw.dma("sp", a.bs[:], P["a_b_s"][j].rearrange("g t -> t g"), writes=[a.b_bs], semkey="bs")
    fw.op("pool", lambda e: e.affine_select(out=a.wsf, in_=a.wsf, pattern=[[0, 16], [-1, 128]],
                                            compare_op=ALU.is_ge, fill=0.0, base=0, channel_multiplier=1),
          reads=[a.b_wsf], writes=[a.b_wsf])
    fw.op("dve", lambda e: e.tensor_copy(out=a.wsb, in_=a.wsf), reads=[a.b_wsf], writes=[a.b_wsb])
    fw.op("dve", lambda e: e.tensor_reduce(out=a.w1[:], in_=a.wsf, axis=AX.X, op=ALU.add), reads=[a.b_wsf], writes=[a.b_w1])
    for g in range(16):
        gs = slice(g * 256, (g + 1) * 256)
        fw.op("dve", lambda e: e.tensor_scalar(out=a.lnb[:, gs], in0=a.lnb[:, gs], scalar1=a.w1[:, g:g + 1], scalar2=a.bs[:, g:g + 1],
                                               op0=ALU.mult, op1=ALU.add), reads=[a.b_lnb, a.b_w1, a.b_bs], writes=[a.b_lnb])
    for half in range(2):
        bank, bb = next_bank(c)
        pv = bank[:].bitcast(BF16).rearrange("p (k t) -> p k t", k=8)
        for k in range(8):
            g = half * 8 + k
            fw.op("pe", lambda e: e.transpose(out=pv[:, k, :], in_=a.wsb[:, g, :], identity=c.idn[:]),
                  reads=[a.b_wsb, c.b_idn], writes=[bb])
        fw.op("dve", lambda e: e.tensor_copy(out=a.wsT[:, half * 8:(half + 1) * 8, :], in_=pv), reads=[bb], writes=[a.b_wsT])


def phase_A(c, a, j, P, x_in, x_out, wb_in, wbufs_in, wb_out, wbufs_out):
    fw = c.fw
    NT = c.NT
    ngroups = NT // 4
    layer_A_consts(c, a, j, P)
    Bm, b_Bm = a.lnb, a.b_lnb

    def rms_tile(gi, i):
        t = gi * 4 + i
        rms_to_hT(c, a, x_in[t * 128:(t + 1) * 128, :], c.bX[t], i, a.hTs[gi % 2], a.b_hTs[gi % 2], a.g_bc, a.b_gbc)

    def mm16(bank, bb, lhs_of_k, W, bW, extra_reads, start_first=True, stop_last=True):
        for k in range(16):
            fw.op("pe", lambda e: e.matmul(bank[:, :], lhsT=lhs_of_k(k), rhs=W[:, k, :],
                                           start=(start_first and k == 0), stop=(stop_last and k == 15)),
                  reads=[bW] + (extra_reads(k) if callable(extra_reads) else extra_reads), writes=[bb])

    for i in range(4):
        rms_tile(0, i)
    for gi in range(ngroups):
        hT, b_hT = a.hTs[gi % 2], a.b_hTs[gi % 2]
        for cidx in range(8):
            W, bW = load_W(c, a, wb_in, wbufs_in, 0, 8 + cidx)
            for i in range(4):
                bank, bb = next_bank(c)
                mm16(bank, bb, lambda k: hT[:, k, i * 128:(i + 1) * 128], W, bW, [b_hT])
                vs = a.v[i][:, cidx * 512:(cidx + 1) * 512]
                fw.op("act", lambda e: e.activation(out=vs, in_=bank[:, :], func=AF.Gelu), reads=[bb], writes=[a.b_v[i][cidx]])
                fw.op("dve", lambda e: e.bn_stats(out=a.stats[i][:, cidx, :], in_=vs), reads=[a.b_v[i][cidx]], writes=[a.b_stats[i]])
        for i in range(4):
            mv, bmv = a.mvs[i], a.b_mvs[i]
            fw.op("dve", lambda e: e.bn_aggr(out=mv[:, 0:2], in_=a.stats[i][:]), reads=[a.b_stats[i]], writes=[bmv])
        for i in range(4):
            mv, bmv = a.mvs[i], a.b_mvs[i]
            fw.op("act", lambda e: e.activation(out=mv[:, 2:3], in_=mv[:, 1:2], func=AF.Sqrt, bias=EPS, scale=1.0),
                  reads=[bmv], writes=[bmv])
        for i in range(4):
            mv, bmv = a.mvs[i], a.b_mvs[i]
            fw.op("dve", lambda e: e.reciprocal(out=mv[:, 3:4], in_=mv[:, 2:3]), reads=[bmv], writes=[bmv])
        for i in range(4):
            mv, bmv = a.mvs[i], a.b_mvs[i]
            fw.op("dve", lambda e: e.tensor_scalar(out=mv[:, 4:5], in0=mv[:, 0:1], scalar1=-1.0, scalar2=mv[:, 3:4],
                                                   op0=ALU.mult, op1=ALU.mult), reads=[bmv], writes=[bmv])
        for i in range(4):
            mv, bmv = a.mvs[i], a.b_mvs[i]
            for q in range(8):
                sl = slice(q * 512, (q + 1) * 512)
                tm, btm = a.tmp32s[a.tmr], a.b_tmp32s[a.tmr]
                a.tmr = (a.tmr + 1) % 3
                fw.op("act", lambda e: e.activation(out=tm[:], in_=a.v[i][:, sl], func=AF.Identity,
                                                    bias=mv[:, 4:5], scale=mv[:, 3:4]),
                      reads=[a.b_v[i][q], bmv], writes=[btm])
                fw.op("dve", lambda e: e.tensor_tensor(out=a.v[i][:, sl], in0=tm[:], in1=a.lng[:, sl], op=ALU.mult),
                      reads=[btm, a.b_lng], writes=[a.b_v[i][q]])
        c.pump.pump(c.pump_rate)
        pend = []

        def flush_tr():
            while pend:
                i_, cidx_, gk = pend.pop(0)
                bank, bb = next_bank(c)
                pv = bank[:].bitcast(BF16)[:, 0:512].rearrange("p (k t) -> p k t", k=4)
                for k in range(4):
                    fw.op("pe", lambda e: e.transpose(out=pv[:, k, :], in_=a.gtc[gk][:, k * 128:(k + 1) * 128], identity=c.idn[:]),
                          reads=[a.b_gtc[gk], c.b_idn], writes=[bb])
                dst = a.v[i_][:, cidx_ * 512:(cidx_ + 1) * 512].rearrange("p (k t) -> p k t", k=4)
                if (i_ + cidx_) % 2 == 0:
                    fw.op("act", lambda e: e.activation(out=dst, in_=pv, func=AF.Copy), reads=[bb], writes=[a.b_v[i_][cidx_]])
                else:
                    fw.op("dve", lambda e: e.tensor_copy(out=dst, in_=pv), reads=[bb], writes=[a.b_v[i_][cidx_]])

        for cidx in range(8):
            Wu, bWu = load_W(c, a, wb_in, wbufs_in, 0, cidx)
            Wz, bWz = load_W(c, a, wb_in, wbufs_in, 0, 16 + cidx)
            pre = {}
            if cidx == 0:
                for i in range(2):
                    pu, bpu = next_bank(c)
                    mm16(pu, bpu, lambda k: hT[:, k, i * 128:(i + 1) * 128], Wu, bWu, [b_hT])
                    pz, bpz = next_bank(c)
                    mm16(pz, bpz, lambda k: hT[:, k, i * 128:(i + 1) * 128], Wz, bWz, [b_hT])
                    pre[i] = (pu, bpu, pz, bpz)
            for i in range(4):
                pm, bpm = next_bank(c)
                for gg in range(2):
                    g = 2 * cidx + gg
                    fw.op("pe", lambda e: e.matmul(pm[:, gg * 256:(gg + 1) * 256], lhsT=a.wsT[:, g, :],
                                                   rhs=a.v[i][:, g * 256:(g + 1) * 256], start=True, stop=True),
                          reads=[a.b_wsT, a.b_v[i][cidx]], writes=[bpm])
                if i in pre:
                    pu, bpu, pz, bpz = pre[i]
                else:
                    pu, bpu = next_bank(c)
                    mm16(pu, bpu, lambda k: hT[:, k, i * 128:(i + 1) * 128], Wu, bWu, [b_hT])
                    pz, bpz = next_bank(c)
                    mm16(pz, bpz, lambda k: hT[:, k, i * 128:(i + 1) * 128], Wz, bWz, [b_hT])
                flush_tr()
                r = a.rr; a.rr = 1 - r
                t1, bt1, t2, bt2, t3, bt3 = a.t1[r], a.b_t1[r], a.t2[r], a.b_t2[r], a.t3[r], a.b_t3[r]
                cs = slice(cidx * 512, (cidx + 1) * 512)
                fw.op("act", lambda e: e.activation(out=t1[:], in_=pu[:, :], func=AF.Gelu), reads=[bpu], writes=[bt1])
                fw.op("act", lambda e: e.activation(out=t2[:], in_=pz[:, :], func=AF.Silu), reads=[bpz], writes=[bt2])
                fw.op("dve", lambda e: e.tensor_tensor(out=t3[:], in0=pm[:, :], in1=Bm[:, cs], op=ALU.add), reads=[bpm, b_Bm], writes=[bt3])
                fw.op("dve", lambda e: e.tensor_tensor(out=t1[:], in0=t1[:], in1=t2[:], op=ALU.mult), reads=[bt1, bt2], writes=[bt1])
                gk = a.gtr; a.gtr = (gk + 1) % 3
                fw.op("dve", lambda e: e.tensor_tensor(out=a.gtc[gk][:], in0=t1[:], in1=t3[:], op=ALU.mult), reads=[bt1, bt3], writes=[a.b_gtc[gk]])
                pend.append((i, cidx, gk))
            if gi + 1 < ngroups:
                tn = (gi + 1) * 4 + cidx // 2
                if cidx % 2 == 0:
                    rms_pre(c, a, x_in[tn * 128:(tn + 1) * 128, :], c.bX[tn], a.g_bc, a.b_gbc)
                else:
                    rms_tr(c, a, cidx // 2, a.hTs[(gi + 1) % 2], a.b_hTs[(gi + 1) % 2])
        flush_tr()
        for nq in range(4):
            pys = [next_bank(c) for _ in range(4)]
            xslots = []
            for i in range(4):
                t = gi * 4 + i
                r = a.xrr; a.xrr = (r + 1) % 4
                xslots.append(r)
                fw.dma("act", a.xr[r][:], x_in[t * 128:(t + 1) * 128, nq * 512:(nq + 1) * 512], reads=[c.bX[t]], writes=[a.b_xr[r]], semkey="xr%d" % r)
            for kh in range(2):
                W, bW = load_W(c, a, wb_out, wbufs_out, kh, nq)
                for i in range(4):
                    py, bpy = pys[i]
                    mm16(py, bpy, lambda k: a.v[i][:, (16 * kh + k) * 128:(16 * kh + k + 1) * 128], W, bW,
                         lambda k: [a.b_v[i][(16 * kh + k) // 4]], start_first=(kh == 0), stop_last=(kh == 1))
            for i in range(4):
                t = gi * 4 + i
                py, bpy = pys[i]
                r = xslots[i]
                rows = slice(t * 128, (t + 1) * 128)
                cols = slice(nq * 512, (nq + 1) * 512)
                fw.op("dve", lambda e: e.tensor_tensor(out=a.xr[r][:], in0=py[:, :], in1=a.xr[r][:], op=ALU.add),
                      reads=[bpy, a.b_xr[r]], writes=[a.b_xr[r]])
                fw.dma("act", x_out[rows, cols], a.xr[r][:], reads=[a.b_xr[r]], writes=[c.bX[t]], semkey="xr%d" % r)


BW = 2048
DIL = (1, 4, 16)
SCALE = float(1.0 / np.sqrt(128.0))
BIG = 30000.0


def alibi_a(g, h):
    n = np.float32(g * 16 + h + 1)
    slope = np.power(np.float32(2.0), np.float32(-8.0) * n / np.float32(48))
    return float(np.float32(slope) * np.float32(DIL[g]))


def convert_B_in(c, w_ap, name):
    nc = c.nc
    wb = nc.dram_tensor(name, [16, 128, 16 * 10 * 128], BF16).ap()
    bufs = []
    end = 0
    for h in range(16):
        b = Buf("%s_%d" % (name, h))
        dst_h = wb[h].rearrange("p (k m n) -> p k m n", k=16, m=10)
        for m in range(10):
            src = w_ap[:, m * 2048 + h * 128: m * 2048 + (h + 1) * 128].rearrange("(k p) n -> p k n", p=128)
            end = c.pump.add(dst_h[:, :, m, :], src, b)
        bufs.append(b)
    return wb, bufs, end


def phase_B0(c, es, j, P, x_in, hTd, b_hTd):
    fw = c.fw
    sb = lambda name, shape, dt: es.enter_context(c.nc.sbuf_tensor(uniq(name), shape, dt))
    aa = []
    for i in range(2):
        a = Ctx()
        a.xt = sb("b0_xt%d" % i, [128, D], F32); a.b_xt = Buf("b0_xt")
        a.hb = sb("b0_hb%d" % i, [128, D], BF16); a.b_hb = Buf("b0_hb")
        a.st = sb("b0_st%d" % i, [128, 16], F32); a.b_st = Buf("b0_st")
        a.xkey = "B0_xt%d" % i
        aa.append(a)
    gbc = sb("b0_gbc", [128, D], F32); b_gbc = Buf("b0_gbc")
    hT = [sb("b0_hT%d" % i, [128, 16, 512], BF16) for i in range(2)]
    b_hT = [Buf("b0_hT%d" % i) for i in range(2)]
    fw.dma("sp", gbc[:], P["b_norm_g"][j].partition_broadcast(128), writes=[b_gbc], semkey="gbc")
    NT = c.NT
    rms_pre(c, aa[0], x_in[0:128, :], c.bX[0], gbc, b_gbc, semkey=aa[0].xkey)
    for t in range(NT):
        gi, i = divmod(t, 4)
        s_ = gi % 2
        if t + 1 < NT:
            rms_pre(c, aa[(t + 1) % 2], x_in[(t + 1) * 128:(t + 2) * 128, :], c.bX[t + 1], gbc, b_gbc, semkey=aa[(t + 1) % 2].xkey)
        rms_tr(c, aa[t % 2], i, hT[s_], b_hT[s_])
        if i == 3:
            fw.dma("sp", hTd[:, :, gi * 512:(gi + 1) * 512], hT[s_][:], reads=[b_hT[s_]], writes=[b_hTd[gi]], semkey="B0_hT%d" % s_)


def phase_B1(c, es, j, wb, wbufs, hTd, b_hTd, ozTd, b_ozTd):
    fw = c.fw
    nc = c.nc
    T = c.T
    NSB = T // 2048
    sb_ = lambda name, shape, dt: es.enter_context(nc.sbuf_tensor(uniq(name), shape, dt))
    Wh = sb_("b1_Wh", [128, 16, 10, 128], BF16); b_Wh = Buf("b1_Wh")
    hTg = [sb_("b1_hTg%d" % i, [128, 16, 512], BF16) for i in range(2)]; b_hTg = [Buf("b1_hTg%d" % i) for i in range(2)]
    KT = [sb_("b1_KT%d" % g, [128, T], BF16) for g in range(3)]
    V = [sb_("b1_V%d" % g, [128, T // 128, 128], BF16) for g in range(3)]
    QT = [sb_("b1_QT%d" % g, [128, 2048], BF16) for g in range(3)]
    sz = sb_("b1_sz", [128, 2048], F32)
    VT2 = sb_("b1_VT2", [128, 2048], BF16); b_VT2 = Buf("b1_VT2")
    VTt = [sb_("b1_VTt%d" % i, [128, 512], BF16) for i in range(2)]; b_VTt = [Buf("b1_VTt%d" % i) for i in range(2)]
    accs = [sb_("b1_acc%d" % i, [128, 2, 2048], F32) for i in range(2)]; b_accs = [Buf("b1_acc%d" % i) for i in range(2)]
    Et = [sb_("b1_E%d" % i, [128, 256], BF16) for i in range(5)]; b_E = [Buf("b1_E%d" % i) for i in range(5)]
    tmp = [sb_("b1_tmp%d" % i, [128, 256], F32) for i in range(5)]; b_tmp = [Buf("b1_tmp%d" % i) for i in range(5)]
    distf = sb_("b1_dist", [128, 256], F32); b_dist = Buf("b1_dist")
    ones = sb_("b1_ones", [128, 128], BF16); b_ones = Buf("b1_ones")
    ob = [sb_("b1_ob%d" % i, [128, 2048], BF16) for i in range(2)]; b_ob = [Buf("b1_ob%d" % i) for i in range(2)]
    fw.op("pool", lambda e: e.memset(ones[:], 1.0), writes=[b_ones])
    fw.op("pool", lambda e: e.iota(distf[:, 0:128], pattern=[[1, 128]], base=128, channel_multiplier=-1, allow_small_or_imprecise_dtypes=True), writes=[b_dist])
    fw.op("pool", lambda e: e.iota(distf[:, 128:256], pattern=[[1, 128]], base=0, channel_multiplier=-1, allow_small_or_imprecise_dtypes=True), writes=[b_dist])
    fw.op("pool", lambda e: e.affine_select(out=distf[:, 0:128], in_=distf[:, 0:128], pattern=[[-1, 128]], compare_op=ALU.is_ge,
                                            fill=BIG, base=0, channel_multiplier=1), reads=[b_dist], writes=[b_dist])
    fw.op("pool", lambda e: e.affine_select(out=distf[:, 128:256], in_=distf[:, 128:256], pattern=[[1, 128]], compare_op=ALU.is_ge,
                                            fill=BIG, base=0, channel_multiplier=-1), reads=[b_dist], writes=[b_dist])
    rr = {"hTg": 0, "vt": 0, "e": 0, "ob": 0, "ev": 0, "acc": 0}
    NG = T // 512
    b_KT = [[Buf("b1_KT%d_%d" % (g, i)) for i in range(NG)] for g in range(3)]
    b_V = [[Buf("b1_V%d_%d" % (g, i)) for i in range(NG)] for g in range(3)]
    b_QT = [[Buf("b1_QT%d_%d" % (g, i)) for i in range(4)] for g in range(3)]
    b_szg = [Buf("b1_sz_%d" % i) for i in range(4)]

    def evac(dst, src_bank, bb, wbuf, func=None):
        if func is not None:
            fw.op("act", lambda e: e.activation(out=dst, in_=src_bank, func=func), reads=[bb], writes=[wbuf])
            return
        rr["ev"] ^= 1
        if rr["ev"]:
            fw.op("act", lambda e: e.activation(out=dst, in_=src_bank, func=AF.Copy), reads=[bb], writes=[wbuf])
        else:
            fw.op("dve", lambda e: e.tensor_copy(out=dst, in_=src_bank), reads=[bb], writes=[wbuf])

    def v_transposes(g, src, b_src, cols_of, blks, wbuf):
        n = len(blks)
        for q0 in range(0, n, 8):
            bank, bb = next_bank(c)
            pv = bank[:].bitcast(BF16).rearrange("p (k t) -> p k t", k=8)
            m = min(8, n - q0)
            for k in range(m):
                fw.op("pe", lambda e: e.transpose(out=pv[:, k, :], in_=cols_of(q0 + k), identity=c.idn[:]),
                      reads=[b_src, c.b_idn], writes=[bb])
            evac(V[g][:, blks[q0]:blks[q0] + m, :], pv[:, 0:m, :], bb, wbuf)

    LAG = 3
    NE = len(Et)
    deferred = []

    def flush_deferred():
        while deferred:
            h_, sbi_, os__ = deferred.pop(0)
            fw.dma("sp", ozTd[h_, :, sbi_ * 2048:(sbi_ + 1) * 2048], ob[os__][:], reads=[b_ob[os__]], writes=[b_ozTd[h_]],
                   semkey="B1_ob%d" % os__)

    for h in range(16):
        fw.dma("sp", Wh[:].rearrange("p k m n -> p (k m n)"), wb[h], reads=[wbufs[h]], writes=[b_Wh], semkey="B1_Wh")
        for sbi in range(NSB):
            ai = rr["acc"]; rr["acc"] ^= 1
            acc, b_acc = accs[ai], b_accs[ai]
            items = []
            st = {"i1": 0, "i2": 0}
            st1 = {}

            def item_info(g, jl, r):
                d = DIL[g]
                span = 128 * d
                nj = 2048 // span
                jg = sbi * nj + jl
                if g == 0:
                    gl_ = jl // 4
                    qb = [b_QT[0][gl_]]
                    kc = [b_KT[0][jg // 4]]; kp = [b_KT[0][(jg - 1) // 4]] if jg > 0 else []
                    vc = [b_V[0][jg // 4]]; vp = [b_V[0][(jg - 1) // 4]] if jg > 0 else []
                elif g == 1:
                    qb = [b_QT[1][jl]]
                    kc = [b_KT[1][jg]]; kp = [b_KT[1][jg - 1]] if jg > 0 else []
                    vc = [b_V[1][jg]]; vp = [b_V[1][jg - 1]] if jg > 0 else []
                else:
                    qb = list(b_QT[2])
                    kc = [b_KT[2][sbi * 4 + i] for i in range(4)]
                    kp = [b_KT[2][(sbi - 1) * 4 + i] for i in range(4)] if jg > 0 else []
                    vc = [b_V[2][sbi]]; vp = [b_V[2][sbi - 1]] if jg > 0 else []
                return d, span, jg, qb, kc, kp, vc, vp

            def stage1(idx):
                g, jl, r = items[idx]
                d, span, jg, qb, kc, kp, vc, vp = item_info(g, jl, r)
                coef = -alibi_a(g, h) / SCALE
                qcols = QT[g][:, jl * span + r: (jl + 1) * span: d]
                kcur = KT[g][:, jg * span + r: (jg + 1) * span: d]
                has_prev = jg > 0
                ps, bps = next_bank(c)
                if has_prev:
                    kprev = KT[g][:, (jg - 1) * span + r: jg * span: d]
                    fw.op("pe", lambda e: e.matmul(ps[:, 0:128], lhsT=kprev, rhs=qcols, start=True, stop=True),
                          reads=kp + qb, writes=[bps])
                fw.op("pe", lambda e: e.matmul(ps[:, 128:256], lhsT=kcur, rhs=qcols, start=True, stop=True),
                      reads=kc + qb, writes=[bps])
                es_ = rr["e"]; rr["e"] = (es_ + 1) % NE
                cs = slice(0, 256) if has_prev else slice(128, 256)
                fw.op("dve", lambda e: e.scalar_tensor_tensor(out=tmp[es_][:, cs], in0=distf[:, cs], scalar=coef, in1=ps[:, cs],
                                                              op0=ALU.mult, op1=ALU.add),
                      reads=[b_dist, bps], writes=[b_tmp[es_]])
                fw.op("act", lambda e: e.activation(out=Et[es_][:, cs], in_=tmp[es_][:, cs], func=AF.Exp, scale=SCALE),
                      reads=[b_tmp[es_]], writes=[b_E[es_]])
                st1[idx] = es_

            def stage2(idx):
                g, jl, r = items[idx]
                d, span, jg, qb, kc, kp, vc, vp = item_info(g, jl, r)
                has_prev = jg > 0
                es_ = st1.pop(idx)
                blk_cur = jg * d + r
                po, bpo = next_bank(c)
                first = True
                for is_ones in (False, True):
                    oc = slice(128, 256) if is_ones else slice(0, 128)
                    if has_prev:
                        l0 = ones[:] if is_ones else V[g][:, blk_cur - d, :]
                        fw.op("pe", lambda e: e.matmul(po[:, oc], lhsT=l0, rhs=Et[es_][:, 0:128], start=first, stop=False,
                                                       skip_group_check=True),
                              reads=([b_ones] if is_ones else vp) + [b_E[es_]], writes=[bpo])
                        first = False
                    l1 = ones[:] if is_ones else V[g][:, blk_cur, :]
                    fw.op("pe", lambda e: e.matmul(po[:, oc], lhsT=l1, rhs=Et[es_][:, 128:256], start=first, stop=True,
                                                   skip_group_check=True),
                          reads=([b_ones] if is_ones else vc) + [b_E[es_]], writes=[bpo])
                    first = False
                pov = po[:, 0:256].rearrange("p (a q) -> p a q", a=2)
                dst = acc[:, :, jl * span + r: (jl + 1) * span: d]
                if g == 0:
                    fw.op("act", lambda e: e.activation(out=dst, in_=pov, func=AF.Copy), reads=[bpo], writes=[b_acc])
                else:
                    fw.op("dve", lambda e: e.tensor_tensor(out=dst, in0=pov, in1=dst, op=ALU.add), reads=[bpo, b_acc], writes=[b_acc])

            def step(drain=False):
                did = False
                if st["i1"] < len(items):
                    stage1(st["i1"]); st["i1"] += 1
                    did = True
                if st["i2"] < st["i1"] and (st["i1"] - st["i2"] > LAG or (drain and st["i1"] == len(items))):
                    stage2(st["i2"]); st["i2"] += 1
                    did = True
                return did

            for gl in range(4):
                gidx = sbi * 4 + gl
                s = rr["hTg"]; rr["hTg"] ^= 1
                fw.dma("sp", hTg[s][:], hTd[:, :, gidx * 512:(gidx + 1) * 512], reads=[b_hTd[gidx]], writes=[b_hTg[s]],
                       semkey="B1_hTg%d" % s)
                if gl == 2:
                    flush_deferred()
                lc = slice(gl * 512, (gl + 1) * 512)
                gc = slice(gidx * 512, (gidx + 1) * 512)
                for m in range(10):
                    bank, bb = next_bank(c)
                    for k in range(16):
                        fw.op("pe", lambda e: e.matmul(bank[:, :], lhsT=Wh[:, k, m, :], rhs=hTg[s][:, k, :], start=(k == 0), stop=(k == 15)),
                              reads=[b_Wh, b_hTg[s]], writes=[bb])
                    g, kind = divmod(m, 3)
                    if m == 9:
                        evac(sz[:, lc], bank[:, :], bb, b_szg[gl], func=AF.Silu)
                    elif kind == 0:
                        evac(QT[g][:, lc], bank[:, :], bb, b_QT[g][gl])
                    elif kind == 1:
                        evac(KT[g][:, gc], bank[:, :], bb, b_KT[g][gidx])
                    else:
                        if g == 2:
                            evac(VT2[:, lc], bank[:, :], bb, b_VT2)
                        else:
                            vs = rr["vt"]; rr["vt"] ^= 1
                            evac(VTt[vs][:], bank[:, :], bb, b_VTt[vs])
                            if g == 0:
                                v_transposes(0, VTt[vs], b_VTt[vs], lambda i: VTt[vs][:, i * 128:(i + 1) * 128],
                                             [gidx * 4 + i for i in range(4)], b_V[0][gidx])
                            else:
                                v_transposes(1, VTt[vs], b_VTt[vs], lambda r: VTt[vs][:, r:512:4],
                                             [gidx * 4 + r for r in range(4)], b_V[1][gidx])
                    step()
                for i in range(4):
                    items.append((0, gl * 4 + i, 0))
                for r in range(4):
                    items.append((1, gl, r))
            v_transposes(2, VT2, b_VT2, lambda r: VT2[:, r:2048:16], [sbi * 16 + r for r in range(16)], b_V[2][sbi])
            for r in range(16):
                items.append((2, 0, r))
            while step(drain=True):
                pass
            assert st["i2"] == len(items) and not st1
            c.pump.pump(c.pump_rate)
            fw.op("dve", lambda e: e.reciprocal(out=acc[:, 1, :], in_=acc[:, 1, :]), reads=[b_acc], writes=[b_acc])
            fw.op("dve", lambda e: e.tensor_tensor(out=acc[:, 0, :], in0=acc[:, 0, :], in1=acc[:, 1, :], op=ALU.mult), reads=[b_acc], writes=[b_acc])
            os_ = rr["ob"]; rr["ob"] ^= 1
            fw.op("dve", lambda e: e.tensor_tensor(out=ob[os_][:], in0=acc[:, 0, :], in1=sz[:], op=ALU.mult), reads=[b_acc] + b_szg, writes=[b_ob[os_]])
            deferred.append((h, sbi, os_))
    flush_deferred()


def phase_B3(c, es, wbo, wbufs_o, ozTd, b_ozTd, x_in, x_out):
    fw = c.fw
    nc = c.nc
    sb_ = lambda name, shape, dt: es.enter_context(nc.sbuf_tensor(uniq(name), shape, dt))
    ozg = [sb_("b3_ozg%d" % i, [128, 16, 512], BF16) for i in range(2)]; b_ozg = [Buf("b3_ozg%d" % i) for i in range(2)]
    W = [sb_("b3_W%d" % i, [128, 16, 512], BF16) for i in range(4)]; b_W = [Buf("b3_W%d" % i) for i in range(4)]
    xr = [sb_("b3_xr%d" % i, [128, 2048], F32) for i in range(3)]; b_xr = [Buf("b3_xr%d" % i) for i in range(3)]
    for nq in range(4):
        fw.dma("sp", W[nq][:].rearrange("p k n -> p (k n)"), wbo[0, nq], reads=[wbufs_o[(0, nq)]], writes=[b_W[nq]], semkey="A_W%d" % (nq % 3) if nq < 3 else "B3_W3")
    rr = 0
    for gi in range(c.NT // 4):
        s = gi % 2
        fw.dma("sp", ozg[s][:], ozTd[:, :, gi * 512:(gi + 1) * 512].rearrange("h p t -> p h t"), reads=b_ozTd, writes=[b_ozg[s]], semkey="B3_ozg%d" % s)
        for i in range(4):
            t = gi * 4 + i
            r = rr; rr = (rr + 1) % 3
            rows = slice(t * 128, (t + 1) * 128)
            fw.dma("act", xr[r][:], x_in[rows, :], reads=[c.bX[t]], writes=[b_xr[r]], semkey="xr%d" % r)
            for nq in range(4):
                bank, bb = next_bank(c)
                for k in range(16):
                    fw.op("pe", lambda e: e.matmul(bank[:, :], lhsT=ozg[s][:, k, i * 128:(i + 1) * 128], rhs=W[nq][:, k, :],
                                                   start=(k == 0), stop=(k == 15)), reads=[b_ozg[s], b_W[nq]], writes=[bb])
                cols = slice(nq * 512, (nq + 1) * 512)
                fw.op("dve", lambda e: e.tensor_tensor(out=xr[r][:, cols], in0=bank[:, :], in1=xr[r][:, cols], op=ALU.add), reads=[bb, b_xr[r]], writes=[b_xr[r]])
            fw.dma("act", x_out[rows, :], xr[r][:], reads=[b_xr[r]], writes=[c.bX[t]], semkey="xr%d" % r)


def phase_final(c, es, P, x_in, x_out):
    fw = c.fw
    sb_ = lambda name, shape, dt: es.enter_context(c.nc.sbuf_tensor(uniq(name), shape, dt))
    xt = [sb_("f_xt%d" % i, [128, D], F32) for i in range(2)]; b_xt = [Buf("f_xt%d" % i) for i in range(2)]
    jk = sb_("f_jk", [128, D], BF16); b_jk = Buf("f_jk")
    st = sb_("f_st", [128, 4], F32); b_st = Buf("f_st")
    gbc = sb_("f_gbc", [128, D], F32); b_gbc = Buf("f_gbc")
    fw.dma("sp", gbc[:], P["final_norm_g"].partition_broadcast(128), writes=[b_gbc], semkey="gbc")
    for t in range(c.NT):
        s = t % 2
        rows = slice(t * 128, (t + 1) * 128)
        fw.dma("sp", xt[s][:], x_in[rows, :], reads=[c.bX[t]], writes=[b_xt[s]], semkey="xr%d" % s)
        fw.op("act", lambda e: e.activation(out=jk[:], in_=xt[s][:], func=AF.Square, scale=float(D ** -0.5), accum_out=st[:, 0:1]),
              reads=[b_xt[s]], writes=[b_jk, b_st])
        fw.op("act", lambda e: e.activation(out=st[:, 1:2], in_=st[:, 0:1], func=AF.Sqrt, bias=EPS, scale=1.0), reads=[b_st], writes=[b_st])
        fw.op("dve", lambda e: e.reciprocal(out=st[:, 2:3], in_=st[:, 1:2]), reads=[b_st], writes=[b_st])
        fw.op("dve", lambda e: e.scalar_tensor_tensor(out=xt[s][:], in0=xt[s][:], scalar=st[:, 2:3], in1=gbc[:], op0=ALU.mult, op1=ALU.mult),
              reads=[b_xt[s], b_st, b_gbc], writes=[b_xt[s]])
        fw.dma("sp", x_out[rows, :], xt[s][:], reads=[b_xt[s]], writes=[c.bX[t]], semkey="xr%d" % s)


T_CORE = 4096
N_CORES = 4
from concourse.bass_utils import run_bass_kernel_spmd


def build_program(n_pairs=2, final=True):
    T = T_CORE
    nc = bass.Bass("TRN2", target_bir_lowering=False)
    P = {}

    def inp(name, shape):
        P[name] = nc.dram_tensor(name, shape, F32, kind="ExternalInput").ap()
    inp("x", [T, 2048])
    inp("a_norm_g", [2, 2048]); inp("a_w_in", [2, 2048, 12288]); inp("a_ln_g", [2, 4096]); inp("a_ln_b", [2, 4096])
    inp("a_w_s", [2, 16, 128, 128]); inp("a_b_s", [2, 16, 128]); inp("a_w_out", [2, 4096, 2048])
    inp("b_norm_g", [2, 2048]); inp("b_w_in", [2, 2048, 20480]); inp("b_w_out", [2, 2048, 2048]); inp("final_norm_g", [2048])
    out = nc.dram_tensor("out", [T, 2048], F32, kind="ExternalOutput").ap()
    hTd = nc.dram_tensor("hTd", [128, 16, T], BF16).ap()
    ozTd = nc.dram_tensor("ozTd", [16, 128, T], BF16).ap()
    es = ExitStack()
    with es:
        fw = FW(nc, es)
        c = setup_common(nc, es, fw, T)
        conv = []
        a_order_in = [(0, 8 + i) for i in range(8)]
        for i in range(8):
            a_order_in += [(0, i), (0, 16 + i)]
        a_order_out = [(kh, nq) for nq in range(4) for kh in range(2)]
        for j in range(2):
            wa_in = convert_weights(c, P["a_w_in"][j], 2048, 12288, "wb_a_in%d" % j, order=a_order_in)
            wa_out = convert_weights(c, P["a_w_out"][j], 4096, 2048, "wb_a_out%d" % j, order=a_order_out)
            wb_in = convert_B_in(c, P["b_w_in"][j], "wb_b_in%d" % j)
            wb_out = convert_weights(c, P["b_w_out"][j], 2048, 2048, "wb_b_out%d" % j)
            conv.append((wa_in, wa_out, wb_in, wb_out))
        x_cur = P["x"]
        for j in range(n_pairs):
            wa_in, wa_out, wb_in, wb_out = conv[j]
            c.pump.ensure(wa_out[2])
            c.pump_rate = 22 if j == 0 else 0
            with ExitStack() as es2:
                a = setup_A(c, es2)
                phase_A(c, a, j, P, x_cur, out, wa_in[0], wa_in[1], wa_out[0], wa_out[1])
                fw.barrier()
            x_cur = out
            c.pump.ensure(wb_out[2])
            c.pump_rate = 7 if j == 0 else 0
            b_hTd = [Buf("hTd%d" % i) for i in range(T // 512)]
            b_ozTd = [Buf("ozTd%d" % i) for i in range(16)]
            with ExitStack() as es2:
                phase_B0(c, es2, j, P, x_cur, hTd, b_hTd)
                fw.barrier()
            with ExitStack() as es2:
                phase_B1(c, es2, j, wb_in[0], wb_in[1], hTd, b_hTd, ozTd, b_ozTd)
                fw.barrier()
            with ExitStack() as es2:
                phase_B3(c, es2, wb_out[0], wb_out[1], ozTd, b_ozTd, x_cur, out)
                fw.barrier()
        if n_pairs == 2:
            c.pump.ensure(len(c.pump.q))
        with ExitStack() as es2:
            phase_final(c, es2, P, out, out)
            fw.barrier()
        fw.finish(c.bX, e="sp")
        fw.finish(c.bX, e="act")
    return nc


WORK_CORES = (0, 1, 4, 5)


def kernel(**inputs):
    x = np.ascontiguousarray(np.asarray(inputs["x"], dtype=np.float32))
    nc = build_program()
    shared = {k: np.ascontiguousarray(np.asarray(v, dtype=np.float32)) for k, v in inputs.items() if k != "x"}
    zeros = {k: np.zeros_like(v) for k, v in shared.items()}
    zeros["x"] = np.zeros_like(x[0])
    in_maps = []
    for core in range(8):
        if core in WORK_CORES:
            m = dict(shared)
            m["x"] = x[WORK_CORES.index(core)]
        else:
            m = zeros
        in_maps.append(m)
    res = run_bass_kernel_spmd(nc, in_maps, core_ids=list(range(8)))
    return np.stack([np.asarray(res.results[core]["out"], dtype=np.float32) for core in WORK_CORES], axis=0)
```
